# Optimizing a Trainium2 kernel written in Bass

```python
import math
import jax, jax.numpy as jnp
from jax import lax
import numpy as np

D_MODEL = 2048
BATCH = 4
SEQ = 4096
DEPTH = 2

N_MEM = 256
HEAD_DIM = 128
N_HEADS_TOTAL = D_MODEL // HEAD_DIM
N_MEM_HEADS = 4
N_SELF_HEADS = N_HEADS_TOTAL - N_MEM_HEADS
SELF_WIDTH = N_SELF_HEADS * HEAD_DIM
MEM_WIDTH = N_MEM_HEADS * HEAD_DIM
DIFF_QK_DIM = HEAD_DIM // 2
DSA_KV_HEADS = 4
DSA_GROUP = N_SELF_HEADS // DSA_KV_HEADS
IDX_HEADS = 16
IDX_DIM = 64
TOPK_MAX = 256
NUM_BUCKETS = 32
MAX_DISTANCE = 128
D_FF = 5632
Q_BLOCK = 128
N_MIXERS = 2
N_A = (DEPTH + 1) // 2
N_B = DEPTH // 2
EPS = 1e-6
NEG_INF = -1e30
A_WIDTHS = (SELF_WIDTH, SELF_WIDTH, SELF_WIDTH, MEM_WIDTH)
B_WIDTHS = (SELF_WIDTH, DSA_KV_HEADS * HEAD_DIM, DSA_KV_HEADS * HEAD_DIM,
            IDX_HEADS * IDX_DIM, IDX_DIM, IDX_HEADS, MEM_WIDTH)
A_IN_WIDTH = sum(A_WIDTHS)
B_IN_WIDTH = sum(B_WIDTHS)

kernel_name = "hybrid_diffattn_dsa_macaron"


def rms_norm(x, g):
    xf = x.astype(jnp.float32)
    y = xf * lax.rsqrt(jnp.mean(xf * xf, axis=-1, keepdims=True) + EPS)
    return (y * g.astype(jnp.float32)).astype(x.dtype)


def swiglu(h, w_gate, w_up, w_down):
    return (jax.nn.silu(h @ w_gate) * (h @ w_up)) @ w_down


def _split(x, widths):
    cuts = [int(c) for c in np.cumsum(widths)[:-1]]
    return jnp.split(x, cuts, axis=-1)


def t5_bucket(dist):
    n = jnp.maximum(dist, 0)
    max_exact = NUM_BUCKETS // 2
    nf = jnp.maximum(n, 1).astype(jnp.float32)
    large = max_exact + (jnp.log(nf / max_exact) / math.log(MAX_DISTANCE / max_exact)
                         * (NUM_BUCKETS - max_exact)).astype(jnp.int32)
    large = jnp.minimum(large, NUM_BUCKETS - 1)
    return jnp.where(n < max_exact, n, large)


def diff_attention(q, k, v, lam, rel_bias):
    B, T, H, _, dq = q.shape
    nb = T // Q_BLOCK
    scale = dq ** -0.5
    kpos = jnp.arange(T)
    qb = jnp.moveaxis(q.reshape(B, nb, Q_BLOCK, H, 2, dq), 1, 0)

    def block(args):
        q_blk, blk = args
        qpos = blk * Q_BLOCK + jnp.arange(Q_BLOCK)
        dist = qpos[:, None] - kpos[None, :]
        bias = jnp.transpose(rel_bias[t5_bucket(dist)], (2, 0, 1)).astype(jnp.float32)
        s = jnp.einsum("bqhmd,bkhmd->bhmqk", q_blk, k).astype(jnp.float32) * scale
        s = jnp.where(dist >= 0, s + bias[None, :, None], NEG_INF)
        p = jax.nn.softmax(s, axis=-1)
        a = p[:, :, 0] - lam * p[:, :, 1]
        return jnp.einsum("bhqk,bkhd->bqhd", a.astype(v.dtype), v)

    out = lax.map(block, (qb, jnp.arange(nb)))
    return jnp.moveaxis(out, 0, 1).reshape(B, T, H, v.shape[-1])


def dsa_attention(q, k, v, iq, ik, iw, rel_bias):
    B, T, H, dh = q.shape
    G = k.shape[2]
    R = H // G
    topk = min(TOPK_MAX, T // 4)
    nb = T // Q_BLOCK
    scale = dh ** -0.5
    idx_scale = IDX_DIM ** -0.5
    w_scale = IDX_HEADS ** -0.5
    kpos = jnp.arange(T)
    qb = jnp.moveaxis(q.reshape(B, nb, Q_BLOCK, G, R, dh), 1, 0)
    iqb = jnp.moveaxis(iq.reshape(B, nb, Q_BLOCK, IDX_HEADS, IDX_DIM), 1, 0)
    iwb = jnp.moveaxis(iw.reshape(B, nb, Q_BLOCK, IDX_HEADS), 1, 0)
    gather = jax.vmap(lambda tb, ib: tb[ib])

    def block(args):
        q_blk, iq_blk, iw_blk, blk = args
        qpos = blk * Q_BLOCK + jnp.arange(Q_BLOCK)
        causal = qpos[:, None] >= kpos[None, :]
        logits = jnp.einsum("bqjd,bkd->bqjk", iq_blk, ik).astype(jnp.float32) * idx_scale
        score = jnp.einsum("bqj,bqjk->bqk", iw_blk.astype(jnp.float32) * w_scale,
                           jax.nn.relu(logits))
        score = jnp.where(causal[None], score, NEG_INF)
        _, idx = lax.top_k(score, topk)
        valid = idx <= qpos[None, :, None]
        k_sel = gather(k, idx)
        v_sel = gather(v, idx)
        s = jnp.einsum("bqgrd,bqkgd->bgrqk", q_blk, k_sel).astype(jnp.float32) * scale
        bias = rel_bias[t5_bucket(qpos[None, :, None] - idx)].astype(jnp.float32)
        bias = jnp.transpose(bias.reshape(B, Q_BLOCK, topk, G, R), (0, 3, 4, 1, 2))
        s = jnp.where(valid[:, None, None], s + bias, NEG_INF)
        p = jax.nn.softmax(s, axis=-1)
        o = jnp.einsum("bgrqk,bqkgd->bqgrd", p.astype(v.dtype), v_sel)
        return o.reshape(B, Q_BLOCK, H, dh)

    out = lax.map(block, (qb, iqb, iwb, jnp.arange(nb)))
    return jnp.moveaxis(out, 0, 1).reshape(B, T, H, dh)


def memory_attention(qm, km, vm):
    scale = qm.shape[-1] ** -0.5
    s = jnp.einsum("bthd,bnhd->bhtn", qm, km).astype(jnp.float32) * scale
    p = jax.nn.softmax(s, axis=-1)
    return jnp.einsum("bhtn,bnhd->bthd", p.astype(vm.dtype), vm)


def _nrm(key, shape, scale):
    return jax.random.normal(key, shape, jnp.float32) * scale


def setup_inputs(seed: int = 0) -> dict:
    key = jax.random.key(seed)
    k = jax.random.split(key, 32)
    D, F = D_MODEL, D_FF
    return {
        "x": _nrm(k[0], (BATCH, SEQ, D), 1.0),
        "mem": _nrm(k[1], (BATCH, N_MEM, D), 1.0),
        "rel_bias": _nrm(k[2], (NUM_BUCKETS, N_SELF_HEADS), 0.2),
        "ffn1_g": 1.0 + _nrm(k[3], (DEPTH, D), 0.02),
        "ffn1_w_gate": _nrm(k[4], (DEPTH, D, F), D ** -0.5),
        "ffn1_w_up": _nrm(k[5], (DEPTH, D, F), D ** -0.5),
        "ffn1_w_down": _nrm(k[6], (DEPTH, F, D), F ** -0.5),
        "ffn2_g": 1.0 + _nrm(k[7], (DEPTH, D), 0.02),
        "ffn2_w_gate": _nrm(k[8], (DEPTH, D, F), D ** -0.5),
        "ffn2_w_up": _nrm(k[9], (DEPTH, D, F), D ** -0.5),
        "ffn2_w_down": _nrm(k[10], (DEPTH, F, D), F ** -0.5),
        "mix_g": 1.0 + _nrm(k[11], (DEPTH, D), 0.02),
        "mem_g": 1.0 + _nrm(k[12], (DEPTH, D), 0.02),
        "mem_w_kv": _nrm(k[13], (DEPTH, D, 2 * MEM_WIDTH), D ** -0.5),
        "mem_gq": 1.0 + _nrm(k[14], (DEPTH, HEAD_DIM), 0.02),
        "mem_gk": 1.0 + _nrm(k[15], (DEPTH, HEAD_DIM), 0.02),
        "w_out": _nrm(k[16], (DEPTH, D, D), D ** -0.5),
        "a_w_in": _nrm(k[17], (N_A, D, A_IN_WIDTH), D ** -0.5),
        "a_gq": 1.0 + _nrm(k[18], (N_A, DIFF_QK_DIM), 0.02),
        "a_gk": 1.0 + _nrm(k[19], (N_A, DIFF_QK_DIM), 0.02),
        "a_lam_q1": _nrm(k[20], (N_A, DIFF_QK_DIM), 0.1),
        "a_lam_k1": _nrm(k[21], (N_A, DIFF_QK_DIM), 0.1),
        "a_lam_q2": _nrm(k[22], (N_A, DIFF_QK_DIM), 0.1),
        "a_lam_k2": _nrm(k[23], (N_A, DIFF_QK_DIM), 0.1),
        "a_g_sub": 1.0 + _nrm(k[24], (N_A, HEAD_DIM), 0.02),
        "b_w_in": _nrm(k[25], (N_B, D, B_IN_WIDTH), D ** -0.5),
        "b_gq": 1.0 + _nrm(k[26], (N_B, HEAD_DIM), 0.02),
        "b_gk": 1.0 + _nrm(k[27], (N_B, HEAD_DIM), 0.02),
    }


def reference(x, mem, rel_bias, ffn1_g, ffn1_w_gate, ffn1_w_up, ffn1_w_down,
              ffn2_g, ffn2_w_gate, ffn2_w_up, ffn2_w_down, mix_g, mem_g, mem_w_kv,
              mem_gq, mem_gk, w_out, a_w_in, a_gq, a_gk, a_lam_q1, a_lam_k1,
              a_lam_q2, a_lam_k2, a_g_sub, b_w_in, b_gq, b_gk):
    B, T, _ = x.shape
    N = mem.shape[1]
    for i in range(DEPTH):
        x = x + 0.5 * swiglu(rms_norm(x, ffn1_g[i]), ffn1_w_gate[i], ffn1_w_up[i], ffn1_w_down[i])

        km, vm = _split(rms_norm(mem, mem_g[i]) @ mem_w_kv[i], (MEM_WIDTH, MEM_WIDTH))
        km = rms_norm(km.reshape(B, N, N_MEM_HEADS, HEAD_DIM), mem_gk[i])
        vm = vm.reshape(B, N, N_MEM_HEADS, HEAD_DIM)

        h = rms_norm(x, mix_g[i])
        if i % N_MIXERS == 0:
            j = i // N_MIXERS
            q, k, v, qm = _split(h @ a_w_in[j], A_WIDTHS)
            q = rms_norm(q.reshape(B, T, N_SELF_HEADS, 2, DIFF_QK_DIM), a_gq[j])
            k = rms_norm(k.reshape(B, T, N_SELF_HEADS, 2, DIFF_QK_DIM), a_gk[j])
            v = v.reshape(B, T, N_SELF_HEADS, HEAD_DIM)
            lambda_init = 0.8 - 0.6 * math.exp(-0.3 * i)
            lam = (jnp.exp(jnp.sum(a_lam_q1[j].astype(jnp.float32) * a_lam_k1[j].astype(jnp.float32)))
                   - jnp.exp(jnp.sum(a_lam_q2[j].astype(jnp.float32) * a_lam_k2[j].astype(jnp.float32)))
                   + lambda_init)
            y_self = diff_attention(q, k, v, lam, rel_bias)
            y_self = rms_norm(y_self, a_g_sub[j]) * (1.0 - lambda_init)
        else:
            j = i // N_MIXERS
            q, k, v, iq, ik, iw, qm = _split(h @ b_w_in[j], B_WIDTHS)
            q = rms_norm(q.reshape(B, T, N_SELF_HEADS, HEAD_DIM), b_gq[j])
            k = rms_norm(k.reshape(B, T, DSA_KV_HEADS, HEAD_DIM), b_gk[j])
            v = v.reshape(B, T, DSA_KV_HEADS, HEAD_DIM)
            iq = iq.reshape(B, T, IDX_HEADS, IDX_DIM)
            y_self = dsa_attention(q, k, v, iq, ik, iw, rel_bias)

        qm = rms_norm(qm.reshape(B, T, N_MEM_HEADS, HEAD_DIM), mem_gq[i])
        y_mem = memory_attention(qm, km, vm)
        y = jnp.concatenate([y_self.reshape(B, T, SELF_WIDTH),
                             y_mem.reshape(B, T, MEM_WIDTH)], axis=-1)
        x = x + y @ w_out[i]

        x = x + 0.5 * swiglu(rms_norm(x, ffn2_g[i]), ffn2_w_gate[i], ffn2_w_up[i], ffn2_w_down[i])
    return x
```

```python
import math
from contextlib import ExitStack

import numpy as np
import ml_dtypes
import concourse.bass as bass
import concourse.mybir as mybir
from concourse.bass_utils import run_bass_kernel_spmd

F32 = mybir.dt.float32
BF16 = mybir.dt.bfloat16
AF = mybir.ActivationFunctionType
ALU = mybir.AluOpType

D = 2048
DC = 16
FF = 5632
T = 4096
NTOK = 2048
NT = 1024
NLT = NTOK // 128
EPS = 1e-6
NEG = -30000.0


class Tok:
    __slots__ = ("name", "w", "r")

    def __init__(self, name=""):
        self.name = name
        self.w = None
        self.r = []


def toks(n):
    return [Tok() for _ in range(n)]


class Op:
    __slots__ = ("eng", "fn", "reads", "writes", "deps", "sig", "sidx", "dma",
                 "dsem", "dval", "id", "eidx", "xdeps", "cc")


COMPUTE = ("pe", "act", "dve", "pool")


class Sched:
    def __init__(self, nc):
        self.nc = nc
        self.ops = []
        self.eng_obj = {"pe": nc.tensor, "act": nc.scalar, "dve": nc.vector,
                        "pool": nc.gpsimd, "sp": nc.sync}
        self.n_dma_sems = {"sp": 8, "act": 4, "pool": 8}
        self.ecount = {e: 0 for e in self.eng_obj}
        self.last = {e: None for e in self.eng_obj}
        self.dma_since = []
        self.pending = {}

    def add(self, eng, fn, reads=(), writes=(), dma=False, cc=False):
        op = Op()
        op.cc = cc
        op.eng = eng
        op.fn = fn
        op.reads = list(reads)
        op.writes = list(writes)
        op.dma = dma
        op.sig = False
        op.sidx = 0
        op.deps = None
        op.dsem = None
        op.dval = 0
        op.id = len(self.ops)
        op.eidx = self.ecount[eng]
        self.ecount[eng] += 1
        op.xdeps = self.pending.pop(eng, ())
        self.ops.append(op)
        if dma:
            self.dma_since.append(op)
        else:
            self.last[eng] = op
        return op

    def barrier(self):
        deps = [o for o in self.last.values() if o is not None] + list(self.dma_since)
        for e in self.eng_obj:
            self.pending[e] = tuple(self.pending.get(e, ())) + tuple(deps)
        self.dma_since = []

    def emit(self, stack):
        nc = self.nc
        for op in self.ops:
            deps = {}
            for t in op.reads:
                if t.w is not None:
                    deps[t.w.id] = (t.w, True)
            for t in op.writes:
                if t.w is not None and t.w.id not in deps:
                    deps[t.w.id] = (t.w, False)
                lastr = {}
                for r in t.r:
                    lastr[(r.eng, r.dma)] = r
                    if r.dma and r.id not in deps:
                        deps[r.id] = (r, False)
                for r in lastr.values():
                    if r.id not in deps:
                        deps[r.id] = (r, False)
            need = []
            for d, raw in deps.values():
                if d is op:
                    continue
                if (not d.dma) and (not op.dma) and d.eng == op.eng:
                    if d.eng == "pe" or not raw:
                        continue
                    if d.eng in ("act", "dve") and op.eidx - d.eidx > 3:
                        continue
                need.append(d)
            for d in op.xdeps:
                if d is not op:
                    need.append(d)
            for d in need:
                if not d.dma:
                    d.sig = True
            op.deps = need
            for t in op.reads:
                t.r.append(op)
            for t in op.writes:
                t.w = op
                t.r = []
        esem = {e: stack.enter_context(nc.semaphore("s_" + e)) for e in COMPUTE}
        dsems = {q: [stack.enter_context(nc.semaphore(f"d_{q}{i}")) for i in range(n)]
                 for q, n in self.n_dma_sems.items()}
        dcnt = {q: [0] * n for q, n in self.n_dma_sems.items()}
        dnext = {q: 0 for q in self.n_dma_sems}
        ecnt = {e: 0 for e in COMPUTE}
        known = {e: {} for e in self.eng_obj}
        nwait = 0
        for op in self.ops:
            eo = self.eng_obj[op.eng]
            kn = known[op.eng]
            waits = {}
            for d in op.deps:
                if d.dma:
                    key, val = d.dsem, d.dval
                else:
                    key, val = esem[d.eng], d.sidx
                nm = id(key)
                if kn.get(nm, 0) >= val:
                    continue
                if nm not in waits or waits[nm][1] < val:
                    waits[nm] = (key, val)
            if op.cc:
                op.dsem = stack.enter_context(nc.semaphore(f"cc{op.id}"))
                op.dval = 1
            elif op.dma:
                q = op.eng
                i = dnext[q]
                dnext[q] = (i + 1) % len(dsems[q])
                sem = dsems[q][i]
                prev = dcnt[q][i] * 16
                dcnt[q][i] += 1
                op.dsem = sem
                op.dval = dcnt[q][i] * 16
                nm = id(sem)
                if prev > 0 and kn.get(nm, 0) < prev:
                    if nm not in waits or waits[nm][1] < prev:
                        waits[nm] = (sem, prev)
            for nm, (key, val) in waits.items():
                eo.wait_ge(key, val)
                kn[nm] = val
                nwait += 1
            if op.fn is None:
                continue
            inst = op.fn(eo)
            if op.cc:
                inst.then_inc(op.dsem)
            elif op.dma:
                inst.then_inc(op.dsem, 16)
            elif op.sig:
                ecnt[op.eng] += 1
                op.sidx = ecnt[op.eng]
                inst.then_inc(esem[op.eng], 1)
        self.stats = dict(n_ops=len(self.ops), n_waits=nwait, sig=dict(ecnt))
        return self.stats


class Ctx:
    def __init__(self, nc, st):
        self.nc = nc
        self.st = st
        self.S = Sched(nc)
        self.ins = {}
        self.outs = {}
        self.out_toks = []
        self.scr_toks = {}
        self.uid = 0
        self.psum = [st.enter_context(nc.psum_tensor(f"ps{i}", [128, 512], F32)) for i in range(8)]
        self.ptok = toks(8)

    def name(self, p):
        self.uid += 1
        return f"{p}{self.uid}"

    def inp(self, name, shape, dtype=F32):
        t = self.nc.dram_tensor(name, list(shape), dtype, kind="ExternalInput").ap()
        self.ins[name] = t
        return t

    def out(self, name, shape, dtype=F32):
        t = self.nc.dram_tensor(name, list(shape), dtype, kind="ExternalOutput").ap()
        self.outs[name] = t
        return t

    def scratch(self, name, shape, dtype):
        return self.nc.dram_tensor(name, list(shape), dtype, kind="Internal").ap()

    def sb(self, stack, name, shape, dtype):
        return stack.enter_context(self.nc.sbuf_tensor(self.name(name), list(shape), dtype))

    def consts(self, cst):
        S = self.S
        st = self.st
        c32 = self.sb(st, "c32", [128, 640], F32)
        self.cbf = self.sb(st, "cbf", [128, 640], BF16)
        self.t_c = Tok()
        t32 = Tok()
        S.add("sp", lambda e: e.dma_start(out=c32[:], in_=cst), writes=[t32], dma=True)
        S.add("dve", lambda e: e.tensor_copy(out=self.cbf[:], in_=c32[:]), reads=[t32], writes=[self.t_c])
        self.ident32 = c32[:, 0:128]
        self.t_c32 = t32
        self.ident = self.cbf[:, 0:128]
        self.onesD = self.cbf[:, 128:256]
        self.blk64 = self.cbf[:, 256:384]
        self.ones128 = self.cbf[:, 384:512]
        self.ones1 = self.cbf[:, 512:640]


def const_array():
    c = np.zeros((128, 640), np.float32)
    c[:, 0:128] = np.eye(128)
    c[:, 128:256] = 1.0 / D
    c[0:64, 256:320] = 1.0 / 64
    c[64:128, 320:384] = 1.0 / 64
    c[:, 384:512] = 1.0 / 128
    c[:, 512:640] = 1.0
    return c


class WT_:
    def __init__(self, ap, tok):
        self.ap = ap
        self.tok = tok


def load_w(C, eng, dst, W, c0, w, t_dst):
    C.S.add(eng, lambda e: e.dma_start(out=dst, in_=W.ap[:, c0:c0 + w].rearrange("(c p) f -> p c f", p=128)),
            reads=[W.tok], writes=[t_dst], dma=True)


def norm_to_hT(C, ph, xT, t_x, gcol, hT, t_h, ntok, pb):
    S = C.S
    nh = (ntok + 511) // 512
    ph = ExitStack()
    sq = C.sb(ph, "sq", [128, DC, 512], BF16)
    rstd = [C.sb(ph, "rstd", [128, 512], F32) for _ in range(2)]
    t_sq = Tok()
    t_r = toks(2)
    pn = C.psum[pb]
    t_pn = C.ptok[pb]
    for hh in range(nh):
        n = min(512, ntok - hh * 512)
        cs = slice(hh * 512, hh * 512 + n)
        S.add("act", lambda e, cs=cs, n=n: e.activation(out=sq[:, :, :n], in_=xT[:, :, cs], func=AF.Square),
              reads=[t_x[k][hh] for k in range(DC)], writes=[t_sq])
        for k in range(DC):
            S.add("pe", lambda e, k=k, n=n: e.matmul(pn[:, :n], lhsT=C.onesD, rhs=sq[:, k, :n],
                                                     start=(k == 0), stop=(k == DC - 1)),
                  reads=[t_sq, C.t_c], writes=[t_pn])
        r = rstd[hh % 2]
        tr = t_r[hh % 2]
        S.add("act", lambda e, r=r, n=n: e.activation(out=r[:, :n], in_=pn[:, :n], func=AF.Sqrt, bias=EPS, scale=1.0),
              reads=[t_pn], writes=[tr])
        S.add("dve", lambda e, r=r, n=n: e.reciprocal(out=r[:, :n], in_=r[:, :n]), reads=[tr], writes=[tr])
        for k in range(DC):
            S.add("dve", lambda e, k=k, cs=cs, r=r, n=n: e.scalar_tensor_tensor(
                out=hT[:, k, cs], in0=xT[:, k, cs], scalar=gcol[:, k:k + 1], in1=r[:, :n],
                op0=ALU.mult, op1=ALU.mult),
                reads=[t_x[k][hh], tr, C.t_par], writes=[t_h[hh]])
    ph.close()
    S.barrier()


def ffn(C, xT, t_x, gcol, wg, wu, wd, ntok):
    S = C.S
    nh = ntok // 512
    with ExitStack() as ph:
        hT = C.sb(ph, "hT", [128, DC, ntok], BF16)
        t_h = toks(nh)
        norm_to_hT(C, ph, xT, t_x, gcol, hT, t_h, ntok, 6)
        wgt = [C.sb(ph, "wg", [128, DC, 256], BF16) for _ in range(2)]
        wut = [C.sb(ph, "wu", [128, DC, 256], BF16) for _ in range(2)]
        wdt = [C.sb(ph, "wd", [128, 2, D], BF16) for _ in range(2)]
        sg = [C.sb(ph, "sg", [128, 512], F32) for _ in range(2)]
        hid = [C.sb(ph, "hid", [128, 2, ntok], BF16) for _ in range(2)]
        t_wg, t_wu, t_wd, t_sg = toks(2), toks(2), toks(2), toks(2)
        t_hid = [[toks(nh) for _ in range(2)] for _ in range(2)]
        NFG = FF // 256

        def load_gu(fg):
            if fg >= NFG:
                return
            s = fg % 2
            f0 = fg * 256
            load_w(C, "pool", wgt[s][:], wg, f0, 256, t_wg[s])
            load_w(C, "pool", wut[s][:], wu, f0, 256, t_wu[s])

        def load_d(fg):
            if fg >= NFG:
                return
            s = fg % 2
            f0 = fg * 256
            S.add("pool", lambda e: e.dma_start(out=wdt[s][:], in_=wd.ap[f0:f0 + 256, :].rearrange("(c p) d -> p c d", p=128)),
                  reads=[wd.tok], writes=[t_wd[s]], dma=True)

        cnt = [0]

        def gu(fg):
            s = fg % 2
            for hh in range(nh):
                cs = slice(hh * 512, (hh + 1) * 512)
                for c in range(2):
                    ps = cnt[0] % 2
                    cnt[0] += 1
                    pg, pu = C.psum[ps], C.psum[2 + ps]
                    tpg, tpu = C.ptok[ps], C.ptok[2 + ps]
                    for k in range(DC):
                        S.add("pe", lambda e, k=k, pg=pg, c=c, cs=cs: e.matmul(
                            pg[:], lhsT=wgt[s][:, k, c * 128:(c + 1) * 128], rhs=hT[:, k, cs],
                            start=(k == 0), stop=(k == DC - 1)),
                            reads=[t_wg[s], t_h[hh]], writes=[tpg])
                    for k in range(DC):
                        S.add("pe", lambda e, k=k, pu=pu, c=c, cs=cs: e.matmul(
                            pu[:], lhsT=wut[s][:, k, c * 128:(c + 1) * 128], rhs=hT[:, k, cs],
                            start=(k == 0), stop=(k == DC - 1)),
                            reads=[t_wu[s], t_h[hh]], writes=[tpu])
                    S.add("act", lambda e, pg=pg, ps=ps: e.activation(out=sg[ps][:], in_=pg[:], func=AF.Silu),
                          reads=[tpg], writes=[t_sg[ps]])
                    S.add("dve", lambda e, pu=pu, ps=ps, c=c, cs=cs: e.tensor_tensor(
                        out=hid[s][:, c, cs], in0=sg[ps][:], in1=pu[:], op=ALU.mult),
                        reads=[t_sg[ps], tpu], writes=[t_hid[s][c][hh]])

        dcnt = [0]

        def down(fg):
            s = fg % 2
            for hh in range(nh):
                cs = slice(hh * 512, (hh + 1) * 512)
                for dc in range(DC):
                    ps = 4 + dcnt[0] % 2
                    dcnt[0] += 1
                    pd, tpd = C.psum[ps], C.ptok[ps]
                    for c in range(2):
                        S.add("pe", lambda e, c=c, pd=pd, dc=dc, cs=cs: e.matmul(
                            pd[:], lhsT=wdt[s][:, c, dc * 128:(dc + 1) * 128], rhs=hid[s][:, c, cs],
                            start=(c == 0), stop=(c == 1)),
                            reads=[t_wd[s], t_hid[s][c][hh]], writes=[tpd])
                    S.add("dve", lambda e, pd=pd, dc=dc, cs=cs: e.scalar_tensor_tensor(
                        out=xT[:, dc, cs], in0=pd[:], scalar=0.5, in1=xT[:, dc, cs], op0=ALU.mult, op1=ALU.add),
                        reads=[tpd, t_x[dc][hh]], writes=[t_x[dc][hh]])

        load_gu(0)
        load_gu(1)
        load_d(0)
        for fg in range(NFG):
            gu(fg)
            if fg >= 1:
                down(fg - 1)
            load_gu(fg + 2)
            load_d(fg + 1)
        down(NFG - 1)
    S.barrier()


def load_x_tokmajor(C, x_dram, xT, t_x, ntok):
    S = C.S
    with ExitStack() as ph:
        xt = [C.sb(ph, "xt", [128, D], F32) for _ in range(2)]
        t_xt = toks(2)
        cnt = 0
        for tt in range(ntok // 128):
            s = tt % 2
            S.add("sp", lambda e, s=s, tt=tt: e.dma_start(out=xt[s][:], in_=x_dram[tt * 128:(tt + 1) * 128, :]),
                  writes=[t_xt[s]], dma=True)
            hh = tt // 4
            for g in range(4):
                pb = cnt % 2
                cnt += 1
                pt, tpt = C.psum[pb], C.ptok[pb]
                for j in range(4):
                    c = g * 4 + j
                    S.add("pe", lambda e, pt=pt, j=j, c=c, s=s: e.transpose(
                        pt[:, j * 128:(j + 1) * 128], xt[s][:, c * 128:(c + 1) * 128], C.ident32),
                        reads=[t_xt[s], C.t_c32], writes=[tpt])
                eng = "act" if g % 2 == 0 else "dve"
                if eng == "act":
                    fn = lambda e, pt=pt, g=g, tt=tt: e.activation(
                        out=xT[:, g * 4:(g + 1) * 4, tt * 128:(tt + 1) * 128],
                        in_=pt[:].rearrange("p (j t) -> p j t", j=4), func=AF.Copy)
                else:
                    fn = lambda e, pt=pt, g=g, tt=tt: e.tensor_copy(
                        out=xT[:, g * 4:(g + 1) * 4, tt * 128:(tt + 1) * 128],
                        in_=pt[:].rearrange("p (j t) -> p j t", j=4))
                S.add(eng, fn, reads=[tpt], writes=[t_x[c][hh] for c in range(g * 4, g * 4 + 4)])
    S.barrier()


def store_x_tokmajor(C, xT, t_x, y_dram, ntok):
    S = C.S
    with ExitStack() as ph:
        xt = [C.sb(ph, "xo", [128, D], F32) for _ in range(2)]
        t_xt = toks(2)
        cnt = 0
        for tt in range(ntok // 128):
            s = tt % 2
            hh = tt // 4
            for g in range(4):
                pb = cnt % 2
                cnt += 1
                pt, tpt = C.psum[pb], C.ptok[pb]
                for j in range(4):
                    c = g * 4 + j
                    S.add("pe", lambda e, pt=pt, j=j, c=c, tt=tt: e.transpose(
                        pt[:, j * 128:(j + 1) * 128], xT[:, c, tt * 128:(tt + 1) * 128], C.ident32),
                        reads=[t_x[c][hh], C.t_c32], writes=[tpt])
                eng = "act" if g % 2 == 0 else "dve"
                if eng == "act":
                    fn = lambda e, pt=pt, g=g, s=s: e.activation(out=xt[s][:, g * 512:(g + 1) * 512], in_=pt[:], func=AF.Copy)
                else:
                    fn = lambda e, pt=pt, g=g, s=s: e.tensor_copy(out=xt[s][:, g * 512:(g + 1) * 512], in_=pt[:])
                S.add(eng, fn, reads=[tpt], writes=[t_xt[s]])
            ot = Tok()
            C.out_toks.append(ot)
            S.add("sp", lambda e, s=s, tt=tt: e.dma_start(out=y_dram[tt * 128:(tt + 1) * 128, :], in_=xt[s][:]),
                  reads=[t_xt[s]], writes=[ot], dma=True)
    S.barrier()


def load_xT(C, src, xT, t_x, ntok, col0):
    for c in range(DC):
        C.S.add("sp", lambda e, c=c: e.dma_start(out=xT[:, c, :], in_=src[c, :, col0:col0 + ntok]),
                writes=[t_x[c][hh] for hh in range(ntok // 512)], dma=True)


def store_xT(C, dst, xT, t_x, ntok, col0):
    for c in range(DC):
        ot = Tok()
        C.out_toks.append(ot)
        C.S.add("sp", lambda e, c=c: e.dma_start(out=dst[c, :, col0:col0 + ntok], in_=xT[:, c, :]),
                reads=[t_x[c][hh] for hh in range(ntok // 512)], writes=[ot], dma=True)


def finish(C):
    C.S.add("sp", None, reads=C.out_toks)
    return C.S.emit(C.st)


PAR = {}
_o = 0
for _i in range(2):
    for _n in ("ffn1_g", "ffn2_g", "mix_g", "mem_g"):
        PAR[(_n, _i)] = _o
        _o += 16
    for _n in ("mem_gq", "mem_gk"):
        PAR[(_n, _i)] = _o
        _o += 1
for _n in ("a_gq2", "a_gk2", "a_gsub", "b_gq", "b_gk"):
    PAR[_n] = _o
    _o += 1
for _n in ("lq1", "lk1", "lq2", "lk2"):
    PAR[_n] = _o
    _o += 64
PAR["b31"] = _o
_o += 12
NPAR = _o
DPAR = {"a_gq2s": 0, "a_gsub_s": 1, "b_gq_s": 2, "neglam": 3, "mem_gq_s0": 4, "mem_gq_s1": 5, "tmp": 6}


def pack_params(inp):
    p = np.zeros((128, NPAR), np.float32)
    for i in range(2):
        for n in ("ffn1_g", "ffn2_g", "mix_g", "mem_g"):
            p[:, PAR[(n, i)]:PAR[(n, i)] + 16] = inp[n][i].reshape(16, 128).T
        p[:, PAR[("mem_gq", i)]] = inp["mem_gq"][i]
        p[:, PAR[("mem_gk", i)]] = inp["mem_gk"][i]
    p[:, PAR["a_gq2"]] = np.concatenate([inp["a_gq"][0], inp["a_gq"][0]])
    p[:, PAR["a_gk2"]] = np.concatenate([inp["a_gk"][0], inp["a_gk"][0]])
    p[:, PAR["a_gsub"]] = inp["a_g_sub"][0]
    p[:, PAR["b_gq"]] = inp["b_gq"][0]
    p[:, PAR["b_gk"]] = inp["b_gk"][0]
    for n, k in (("lq1", "a_lam_q1"), ("lk1", "a_lam_k1"), ("lq2", "a_lam_q2"), ("lk2", "a_lam_k2")):
        p[:, PAR[n]:PAR[n] + 64] = np.broadcast_to(inp[k][0][None, :], (128, 64))
    p[:, PAR["b31"]:PAR["b31"] + 12] = np.broadcast_to(inp["rel_bias"][31][None, :], (128, 12))
    return p


def load_params(C, par_dram):
    S = C.S
    st = C.st
    C.par = C.sb(st, "par", [128, NPAR], F32)
    C.dpar = C.sb(st, "dpar", [128, 8], F32)
    C.t_par = Tok()
    t0 = Tok()
    S.add("sp", lambda e: e.dma_start(out=C.par[:], in_=par_dram), writes=[t0], dma=True)
    P, Dp = C.par, C.dpar

    def col(n):
        return P[:, PAR[n]:PAR[n] + 1]

    def dcol(n):
        return Dp[:, DPAR[n]:DPAR[n] + 1]
    C.col = col
    C.dcol = dcol
    S.add("dve", lambda e: e.tensor_scalar(out=dcol("a_gq2s"), in0=col("a_gq2"), scalar1=0.125, scalar2=None, op0=ALU.mult), reads=[t0], writes=[C.t_par])
    S.add("dve", lambda e: e.tensor_scalar(out=dcol("a_gsub_s"), in0=col("a_gsub"), scalar1=0.8, scalar2=None, op0=ALU.mult), reads=[t0], writes=[C.t_par])
    sc = 128 ** -0.5
    S.add("dve", lambda e: e.tensor_scalar(out=dcol("b_gq_s"), in0=col("b_gq"), scalar1=sc, scalar2=None, op0=ALU.mult), reads=[t0], writes=[C.t_par])
    for i in range(2):
        S.add("dve", lambda e, i=i: e.tensor_scalar(out=dcol(f"mem_gq_s{i}"), in0=P[:, PAR[("mem_gq", i)]:PAR[("mem_gq", i)] + 1],
                                                    scalar1=sc, scalar2=None, op0=ALU.mult), reads=[t0], writes=[C.t_par])
    tmp = C.sb(st, "lamtmp", [128, 64], F32)
    s12 = C.sb(st, "lams", [128, 2], F32)
    tl = Tok()
    for j, (a, b) in enumerate((("lq1", "lk1"), ("lq2", "lk2"))):
        S.add("dve", lambda e, a=a, b=b: e.tensor_tensor(out=tmp[:], in0=P[:, PAR[a]:PAR[a] + 64], in1=P[:, PAR[b]:PAR[b] + 64], op=ALU.mult),
              reads=[t0], writes=[tl])
        S.add("dve", lambda e, j=j: e.reduce_sum(out=s12[:, j:j + 1], in_=tmp[:], axis=mybir.AxisListType.X), reads=[tl], writes=[tl])
    S.add("act", lambda e: e.activation(out=s12[:], in_=s12[:], func=AF.Exp), reads=[tl], writes=[tl])
    S.add("dve", lambda e: e.scalar_tensor_tensor(out=dcol("neglam"), in0=s12[:, 1:2], scalar=-0.2, in1=s12[:, 0:1], op0=ALU.add, op1=ALU.subtract),
          reads=[tl], writes=[C.t_par])


class NormRes:
    def __init__(self, C, ph, banks=(3, 4)):
        self.sq = [C.sb(ph, "nsq", [128, 512], BF16) for _ in range(2)]
        self.r = [C.sb(ph, "nr", [128, 512], F32) for _ in range(2)]
        self.t_sq = toks(2)
        self.t_r = toks(2)
        self.banks = banks
        self.i = 0


def post_norm(C, R, src, t_src, n, normmat, gcol, dst, t_dst, src_is_psum=True):
    S = C.S
    i = R.i
    R.i += 1
    s = i % 2
    pb = R.banks[i % len(R.banks)]
    pn, tpn = C.psum[pb], C.ptok[pb]
    sq, r = R.sq[s], R.r[s]
    S.add("act", lambda e: e.activation(out=sq[:, :n], in_=src, func=AF.Square), reads=[t_src], writes=[R.t_sq[s]])
    S.add("pe", lambda e: e.matmul(pn[:, :n], lhsT=normmat, rhs=sq[:, :n], start=True, stop=True),
          reads=[R.t_sq[s], C.t_c], writes=[tpn])
    S.add("act", lambda e: e.activation(out=r[:, :n], in_=pn[:, :n], func=AF.Sqrt, bias=EPS, scale=1.0), reads=[tpn], writes=[R.t_r[s]])
    S.add("dve", lambda e: e.reciprocal(out=r[:, :n], in_=r[:, :n]), reads=[R.t_r[s]], writes=[R.t_r[s]])
    S.add("dve", lambda e: e.scalar_tensor_tensor(out=dst, in0=src, scalar=gcol, in1=r[:, :n], op0=ALU.mult, op1=ALU.mult),
          reads=[t_src, R.t_r[s], C.t_par], writes=[t_dst])


def proj_fm(C, wt, t_wt, W, col_groups, hT, t_h, ntok, post, banks=(0, 1, 2)):
    S = C.S
    nh = (ntok + 511) // 512
    cnt = 0
    ci = 0

    def load(g):
        c0, w = col_groups[g]
        load_w(C, "pool", wt[g % 2][:, :, :w], W, c0, w, t_wt[g % 2])

    load(0)
    for g, (c0, w) in enumerate(col_groups):
        if g + 1 < len(col_groups):
            load(g + 1)
        s = g % 2
        for c in range(w // 128):
            for hh in range(nh):
                n = min(512, ntok - hh * 512)
                cs = slice(hh * 512, hh * 512 + n)
                pb = banks[cnt % len(banks)]
                cnt += 1
                pp, tpp = C.psum[pb], C.ptok[pb]
                for k in range(DC):
                    S.add("pe", lambda e, k=k, pp=pp, c=c, cs=cs, n=n, s=s: e.matmul(
                        pp[:, :n], lhsT=wt[s][:, k, c * 128:(c + 1) * 128], rhs=hT[:, k, cs],
                        start=(k == 0), stop=(k == DC - 1)),
                        reads=[t_wt[s], t_h[hh]], writes=[tpp])
                post(ci, hh, n, pp[:, :n], tpp)
            ci += 1


def proj_tm(C, wv, t_wv, W, col_groups, hT, t_h, ntok, post, banks=(5, 6)):
    S = C.S
    cnt = 0

    def load(g):
        c0, w = col_groups[g]
        load_w(C, "pool", wv[g % 2][:, :, :w], W, c0, w, t_wv[g % 2])

    load(0)
    for g, (c0, w) in enumerate(col_groups):
        if g + 1 < len(col_groups):
            load(g + 1)
        s = g % 2
        for tt in range(ntok // 128):
            pb = banks[cnt % len(banks)]
            cnt += 1
            pp, tpp = C.psum[pb], C.ptok[pb]
            hh = tt // 4
            for k in range(DC):
                S.add("pe", lambda e, k=k, pp=pp, tt=tt, s=s, w=w: e.matmul(
                    pp[:, :w], lhsT=hT[:, k, tt * 128:(tt + 1) * 128], rhs=wv[s][:, k, :w],
                    start=(k == 0), stop=(k == DC - 1)),
                    reads=[t_wv[s], t_h[hh]], writes=[tpp])
            post(g, tt, w, pp[:, :w], tpp)


def mem_kv(C, lay, mem_dram, wkv, layer):
    S = C.S
    kmT = [C.sb(lay, "kmT", [128, 256], BF16) for _ in range(4)]
    vm = [C.sb(lay, "vm", [128, 512], BF16) for _ in range(2)]
    t_km, t_vm = toks(4), toks(2)
    with ExitStack() as ph:
        mT = C.sb(ph, "mT", [128, DC, 256], F32)
        t_m = [toks(1) for _ in range(DC)]
        load_x_tokmajor(C, mem_dram, mT, t_m, 256)
        hT = C.sb(ph, "mhT", [128, DC, 256], BF16)
        t_h = toks(1)
        g0 = PAR[("mem_g", layer)]
        norm_to_hT(C, ph, mT, t_m, C.par[:, g0:g0 + 16], hT, t_h, 256, 7)
        wt = [C.sb(ph, "mwt", [128, DC, 256], BF16) for _ in range(2)]
        t_wt = toks(2)
        R = NormRes(C, ph)
        gk = C.par[:, PAR[("mem_gk", layer)]:PAR[("mem_gk", layer)] + 1]

        def post_k(ci, hh, n, pp, tpp):
            post_norm(C, R, pp, tpp, n, C.ones128, gk, kmT[ci][:, :n], t_km[ci])
        proj_fm(C, wt, t_wt, wkv, [(0, 256), (256, 256)], hT, t_h, 256, post_k)
        wv = [C.sb(ph, "mwv", [128, DC, 512], BF16) for _ in range(2)]
        t_wv = toks(2)

        def post_v(g, tt, w, pp, tpp):
            S.add("act", lambda e: e.activation(out=vm[tt][:], in_=pp, func=AF.Copy), reads=[tpp], writes=[t_vm[tt]])
        proj_tm(C, wv, t_wv, wkv, [(512, 512)], hT, t_h, 256, post_v)
    S.barrier()
    return kmT, vm, t_km, t_vm


def mem_attn(C, ph, qm, t_qm, kmT, vm, t_km, t_vm, ntok, ydst, t_y):
    S = C.S
    E = [C.sb(ph, "mE", [128, 512], BF16) for _ in range(2)]
    rr = C.sb(ph, "mrr", [128, 512], F32)
    t_E = toks(2)
    t_rr = Tok()
    cnt = 0
    for hm in range(4):
        for hh in range(ntok // 512):
            cs = slice(hh * 512, (hh + 1) * 512)
            pnum, tnum = C.psum[5], C.ptok[5]
            pden, tden = C.psum[6], C.ptok[6]
            for kt in range(2):
                pb = cnt % 2
                s = cnt % 2
                cnt += 1
                ps, tps = C.psum[pb], C.ptok[pb]
                S.add("pe", lambda e, ps=ps, hm=hm, kt=kt, cs=cs: e.matmul(
                    ps[:], lhsT=kmT[hm][:, kt * 128:(kt + 1) * 128], rhs=qm[hm][:, cs], start=True, stop=True),
                    reads=[t_km[hm], t_qm[hm]], writes=[tps])
                S.add("act", lambda e, ps=ps, s=s: e.activation(out=E[s][:], in_=ps[:], func=AF.Exp), reads=[tps], writes=[t_E[s]])
                S.add("pe", lambda e, s=s, hm=hm, kt=kt, pnum=pnum: e.matmul(
                    pnum[:], lhsT=vm[kt][:, hm * 128:(hm + 1) * 128], rhs=E[s][:], start=(kt == 0), stop=(kt == 1)),
                    reads=[t_vm[kt], t_E[s]], writes=[tnum])
                S.add("pe", lambda e, s=s, kt=kt, pden=pden: e.matmul(
                    pden[:], lhsT=C.ones1, rhs=E[s][:], start=(kt == 0), stop=(kt == 1)),
                    reads=[t_E[s], C.t_c], writes=[tden])
            S.add("dve", lambda e, pden=pden: e.reciprocal(out=rr[:], in_=pden[:]), reads=[tden], writes=[t_rr])
            S.add("dve", lambda e, pnum=pnum, hm=hm, cs=cs: e.tensor_tensor(out=ydst(hm, cs), in0=pnum[:], in1=rr[:], op=ALU.mult),
                  reads=[tnum, t_rr], writes=[t_y(hm, hh)])


def in_proj_A(C, xT, t_x, col0, w_in, kmv, qT_o, kT_o, V_o, ymT_o):
    S = C.S
    kmT, vm, t_km, t_vm = kmv
    nh = NT // 512
    with ExitStack() as ph:
        hT = C.sb(ph, "ihT", [128, DC, NT], BF16)
        t_h = toks(nh)
        g0 = PAR[("mix_g", 0)]
        norm_to_hT(C, ph, xT, t_x, C.par[:, g0:g0 + 16], hT, t_h, NT, 7)
        wt = [C.sb(ph, "iwt", [128, DC, 256], BF16) for _ in range(2)]
        t_wt = toks(2)
        R = NormRes(C, ph)
        stg = [C.sb(ph, "istg", [128, NT], BF16) for _ in range(2)]
        t_stg = toks(2)
        qm = [C.sb(ph, "iqm", [128, NT], BF16) for _ in range(4)]
        t_qm = toks(4)

        def post(ci, hh, n, pp, tpp):
            cs = slice(hh * 512, hh * 512 + n)
            if ci < 24:
                s = ci % 2
                gcol = C.dcol("a_gq2s") if ci < 12 else C.col("a_gk2")
                post_norm(C, R, pp, tpp, n, C.blk64, gcol, stg[s][:, cs], t_stg[s])
                if hh == nh - 1:
                    dst = qT_o[ci, :, col0:col0 + NT] if ci < 12 else kT_o[ci - 12, :, col0:col0 + NT]
                    ot = Tok()
                    C.out_toks.append(ot)
                    S.add("sp", lambda e: e.dma_start(out=dst, in_=stg[s][:]), reads=[t_stg[s]], writes=[ot], dma=True)
            else:
                hm = ci - 24
                post_norm(C, R, pp, tpp, n, C.ones128, C.dcol("mem_gq_s0"), qm[hm][:, cs], t_qm[hm])

        groups = [(c0, 256) for c0 in range(0, 3072, 256)] + [(4608, 256), (4864, 256)]
        proj_fm(C, wt, t_wt, w_in, groups, hT, t_h, NT, post)

        wv = [C.sb(ph, "iwv", [128, DC, 512], BF16) for _ in range(2)]
        t_wv = toks(2)
        vst = [C.sb(ph, "ivst", [128, 512], BF16) for _ in range(2)]
        t_vst = toks(2)
        vc = [0]

        def post_v(g, tt, w, pp, tpp):
            s = vc[0] % 2
            vc[0] += 1
            S.add("act", lambda e: e.activation(out=vst[s][:], in_=pp, func=AF.Copy), reads=[tpp], writes=[t_vst[s]])
            ot = Tok()
            C.out_toks.append(ot)
            r0 = col0 + tt * 128
            S.add("sp", lambda e: e.dma_start(out=V_o[g * 4:(g + 1) * 4, r0:r0 + 128, :].rearrange("h t c -> t h c"),
                                              in_=vst[s][:].rearrange("t (h c) -> t h c", h=4)),
                  reads=[t_vst[s]], writes=[ot], dma=True)
        proj_tm(C, wv, t_wv, w_in, [(3072, 512), (3584, 512), (4096, 512)], hT, t_h, NT, post_v)

        ym = [C.sb(ph, "iym", [128, NT], BF16) for _ in range(4)]
        t_ym = toks(4)
        mem_attn(C, ph, qm, t_qm, kmT, vm, t_km, t_vm, NT, lambda hm, cs: ym[hm][:, cs], lambda hm, hh: t_ym[hm])
        for hm in range(4):
            ot = Tok()
            C.out_toks.append(ot)
            S.add("sp", lambda e, hm=hm: e.dma_start(out=ymT_o[hm, :, col0:col0 + NT], in_=ym[hm][:]),
                  reads=[t_ym[hm]], writes=[ot], dma=True)
    S.barrier()


def own_rows(a, hf):
    return np.ascontiguousarray(a.reshape(16, 2, 128, *a.shape[1:])[:, hf].reshape(NTOK, *a.shape[1:]))


def load_seq(C, eng, dst, src_all, t_dst):
    for r in range(2):
        C.S.add(eng, lambda e, r=r: e.dma_start(
            out=dst.rearrange("p (l r t) -> p l r t", r=2, t=128)[:, :, r, :],
            in_=src_all[r].rearrange("p (l t) -> p l t", t=128)),
            writes=[t_dst], dma=True)


def load_vseq(C, eng, dst, v_of_rank, t_dst):
    for r in range(2):
        C.S.add(eng, lambda e, r=r: e.dma_start(
            out=dst.rearrange("p (l r) w -> p l r w", r=2)[:, :, r, :],
            in_=v_of_rank(r).rearrange("(l p) w -> p l w", p=128)),
            writes=[t_dst], dma=True)


def prep_wt(C, st, wt_dram):
    S = C.S
    C.WTs = C.sb(st, "wts", [128, 3, 12, 128], BF16)
    C.t_wt = Tok()
    with ExitStack() as tmp:
        w32 = C.sb(tmp, "wt32", [128, 3, 12, 128], F32)
        t0 = Tok()
        S.add("sp", lambda e: e.dma_start(out=w32[:], in_=wt_dram), writes=[t0], dma=True)
        for h in range(12):
            S.add("dve", lambda e, h=h: e.tensor_scalar(
                out=C.WTs[:, :, h, :], in0=w32[:, :, h, :], scalar1=C.par[:, PAR["b31"] + h:PAR["b31"] + h + 1],
                scalar2=None, op0=ALU.subtract), reads=[t0, C.t_par], writes=[C.t_wt])
    S.barrier()


def chunk_keys(ci):
    for jk in range(8 * ci + 8):
        yield jk, max(0, (jk) // 2 - 4 * ci)


def window_adds(C, ps, tps, jk, ci, h, istart):
    n = 0
    for w in range(3):
        if (jk + 1 - w) % 2:
            continue
        i = (jk + 1 - w) // 2 - 4 * ci
        if i < istart or i > 3:
            continue
        C.S.add("pe", lambda e, i=i, w=w: e.matmul(ps[:, i * 128:(i + 1) * 128], lhsT=C.ident, rhs=C.WTs[:, w, h, :],
                                                   start=False, stop=False, skip_group_check=True),
                reads=[C.t_wt, C.t_c], writes=[tps])
        n += 1
    return n


def diff_attention(C, qT_s, kT_all, V_all, yT_s):
    S = C.S
    with ExitStack() as ph:
        kh = [C.sb(ph, "kh", [128, T], BF16) for _ in range(2)]
        vh = [C.sb(ph, "vh", [128, 32, 128], BF16) for _ in range(2)]
        qh = [C.sb(ph, "qh", [128, NTOK], BF16) for _ in range(2)]
        t_kh, t_vh, t_qh = toks(2), toks(2), toks(2)
        E = [C.sb(ph, "E", [128, 512], BF16) for _ in range(3)]
        t_E = toks(3)
        o = [C.sb(ph, "o", [128, 512], F32) for _ in range(2)]
        rr = [C.sb(ph, "rr", [128, 512], F32) for _ in range(2)]
        t_o, t_rr = toks(2), toks(2)
        of = C.sb(ph, "of", [128, 512], F32)
        t_of = Tok()
        R = NormRes(C, ph, banks=(7,))
        yst = [C.sb(ph, "yst", [128, 512], BF16) for _ in range(2)]
        t_yst = toks(2)
        cnt = 0
        fin = 0

        def load_head(h):
            s = h % 2
            load_seq(C, "sp", kh[s][:], kT_all[:, h], t_kh[s])
            load_vseq(C, "sp", vh[s][:], lambda r, h=h: V_all[h, r], t_vh[s])
            S.add("sp", lambda e: e.dma_start(out=qh[s][:], in_=qT_s[h]), writes=[t_qh[s]], dma=True)

        load_head(0)
        for h in range(12):
            if h + 1 < 12:
                load_head(h + 1)
            s = h % 2
            for ci in range(4):
                q0 = ci * 512
                keys = list(chunk_keys(ci))
                for jk, istart in keys:
                    c0 = istart * 128
                    n = 512 - c0
                    for m in range(2):
                        pb = cnt % 3
                        cnt += 1
                        ps, tps = C.psum[pb], C.ptok[pb]
                        ms = slice(m * 64, (m + 1) * 64)
                        S.add("pe", lambda e, ps=ps, ms=ms, jk=jk, c0=c0, s=s, q0=q0: e.matmul(
                            ps[:, c0:512], lhsT=kh[s][ms, jk * 128:(jk + 1) * 128], rhs=qh[s][ms, q0 + c0:q0 + 512],
                            start=True, stop=False, skip_group_check=True),
                            reads=[t_kh[s], t_qh[s]], writes=[tps])
                        window_adds(C, ps, tps, jk, ci, h, istart)
                        Es, tE = E[pb], t_E[pb]
                        S.add("act", lambda e, ps=ps, Es=Es, c0=c0, n=n, h=h: e.activation(
                            out=Es[:, :n], in_=ps[:, c0:512], func=AF.Exp,
                            bias=C.par[:, PAR["b31"] + h:PAR["b31"] + h + 1], scale=1.0),
                            reads=[tps, C.t_par], writes=[tE])
                        pnum, tnum = C.psum[3 + m], C.ptok[3 + m]
                        pden, tden = C.psum[5 + m], C.ptok[5 + m]
                        first, last = (jk == 0), (jk == keys[-1][0])
                        S.add("pe", lambda e, pnum=pnum, Es=Es, jk=jk, c0=c0, n=n, first=first, last=last, s=s: e.matmul(
                            pnum[:, c0:512], lhsT=vh[s][:, jk, :], rhs=Es[:, :n], start=first, stop=last, skip_group_check=True),
                            reads=[t_vh[s], tE], writes=[tnum])
                        S.add("pe", lambda e, pden=pden, Es=Es, c0=c0, n=n, first=first, last=last: e.matmul(
                            pden[:, c0:512], lhsT=C.ones1, rhs=Es[:, :n], start=first, stop=last, skip_group_check=True),
                            reads=[tE, C.t_c], writes=[tden])
                for m in range(2):
                    S.add("dve", lambda e, m=m: e.reciprocal(out=rr[m][:], in_=C.psum[5 + m][:]), reads=[C.ptok[5 + m]], writes=[t_rr[m]])
                    S.add("dve", lambda e, m=m: e.tensor_tensor(out=o[m][:], in0=C.psum[3 + m][:], in1=rr[m][:], op=ALU.mult),
                          reads=[C.ptok[3 + m], t_rr[m]], writes=[t_o[m]])
                S.add("dve", lambda e: e.scalar_tensor_tensor(out=of[:], in0=o[1][:], scalar=C.dcol("neglam"), in1=o[0][:],
                                                              op0=ALU.mult, op1=ALU.add),
                      reads=[t_o[0], t_o[1], C.t_par], writes=[t_of])
                ys = fin % 2
                fin += 1
                post_norm(C, R, of[:], t_of, 512, C.ones128, C.dcol("a_gsub_s"), yst[ys][:], t_yst[ys])
                ot = Tok()
                C.scr_toks.setdefault("yT", []).append(ot)
                S.add("sp", lambda e, ys=ys, q0=q0, h=h: e.dma_start(out=yT_s[h, :, q0:q0 + 512], in_=yst[ys][:]),
                      reads=[t_yst[ys]], writes=[ot], dma=True)
    S.barrier()


def out_proj(C, xT, t_x, yT_s, col0, w_out):
    S = C.S
    nh = NT // 512
    with ExitStack() as ph:
        yT = C.sb(ph, "oyT", [128, DC, NT], BF16)
        t_y = toks(nh)
        for c in range(DC):
            S.add("sp", lambda e, c=c: e.dma_start(out=yT[:, c, :], in_=yT_s[c, :, col0:col0 + NT]), writes=t_y, dma=True)
        wt = [C.sb(ph, "owt", [128, DC, 256], BF16) for _ in range(2)]
        t_wt = toks(2)

        def post(ci, hh, n, pp, tpp):
            cs = slice(hh * 512, hh * 512 + n)
            S.add("dve", lambda e: e.tensor_tensor(out=xT[:, ci, cs], in0=pp, in1=xT[:, ci, cs], op=ALU.add),
                  reads=[tpp, t_x[ci][hh]], writes=[t_x[ci][hh]])
        proj_fm(C, wt, t_wt, w_out, [(c0, 256) for c0 in range(0, D, 256)], yT, t_y, NT, post)
    S.barrier()


PAIRS = [[0, 1], [2, 3], [4, 5], [6, 7]]
ALL8 = [list(range(8))]

WEIGHTS = [
    ("ffn1_w_gate", 0, D, FF), ("ffn1_w_up", 0, D, FF), ("ffn1_w_down", 0, FF, D),
    ("mem_w_kv", 0, D, 1024), ("a_w_in", 0, D, 5120), ("w_out", 0, D, D),
    ("ffn2_w_gate", 0, D, FF), ("ffn2_w_up", 0, D, FF), ("ffn2_w_down", 0, FF, D),
    ("ffn1_w_gate", 1, D, FF), ("ffn1_w_up", 1, D, FF), ("ffn1_w_down", 1, FF, D),
    ("mem_w_kv", 1, D, 1024), ("b_w_in", 0, D, 4176), ("w_out", 1, D, D),
    ("ffn2_w_gate", 1, D, FF), ("ffn2_w_up", 1, D, FF), ("ffn2_w_down", 1, FF, D),
]


def wname(k, i):
    return f"{k}_{i}"


def dram_copy(C, dst, src, rows, t_dst, nsplit=4, reads=()):
    step = (rows + nsplit - 1) // nsplit
    for r0 in range(0, rows, step):
        r1 = min(rows, r0 + step)
        C.S.add("sp", lambda e, r0=r0, r1=r1: e.dma_start(out=dst[r0:r1, :], in_=src[r0:r1, :]),
                reads=list(reads), writes=[t_dst], dma=True)


def allgather(C, src, dst, groups, t_src, t_dst):
    C.S.add("pool", lambda e: e.collective_compute("AllGather", ALU.bypass, replica_groups=groups,
                                                   ins=[src.opt()], outs=[dst.opt()]),
            reads=[t_src], writes=[t_dst], dma=True, cc=True)


def gather_weights(C, nlayers):
    W = {}
    prev = None
    for k, i, rows, cols in WEIGHTS:
        if i >= nlayers or (k == "b_w_in" and nlayers < 2):
            continue
        nm = wname(k, i)
        rs = rows // 8
        shard = C.inp(nm, [rs, cols])
        src = C.scratch(nm + "_s", [rs, cols], BF16)
        full = C.nc.dram_tensor(nm + "_f", [rows, cols], BF16, kind="Internal", addr_space="Shared").ap()
        t_s, t_f = Tok(), Tok()
        step = 64
        for r0 in range(0, rs, step):
            r1 = min(rs, r0 + step)
            C.S.add("pool", lambda e, r0=r0, r1=r1, src=src, shard=shard: e.dma_start(out=src[r0:r1, :], in_=shard[r0:r1, :]),
                    writes=[t_s], dma=True)
        C.S.add("pool", lambda e, src=src, full=full: e.collective_compute(
            "AllGather", ALU.bypass, replica_groups=ALL8, ins=[src.opt()], outs=[full.opt()]),
            reads=[t_s] + ([prev] if prev else []), writes=[t_f], dma=True, cc=True)
        prev = t_f
        W[nm] = WT_(full, t_f)
    return W


def build_program(nlayers=2, stop=0):
    nc = bass.Bass("TRN2", target_bir_lowering=False)
    with ExitStack() as st:
        C = Ctx(nc, st)
        S = C.S
        x = C.inp("x", [NTOK, D])
        mem = C.inp("mem", [256, D])
        cst = C.inp("cst", [128, 640])
        par = C.inp("par", [128, NPAR])
        wtab = C.inp("wtab", [128, 3, 12, 128])
        cmt = C.inp("cmt", [128, 3, 128])
        y = C.out("y", [NTOK, D])
        C.consts(cst)
        load_params(C, par)
        prep_wt(C, st, wtab)
        W = gather_weights(C, nlayers)
        xT_s = C.scratch("xT_s", [DC, 128, NTOK], F32)
        qT_s = C.scratch("qT_s", [12, 128, NTOK], BF16)
        yT_s = C.scratch("yT_s", [DC, 128, NTOK], BF16)
        kx0 = C.scratch("kx0", [12 * 128, NTOK], BF16)
        vx0 = C.scratch("vx0", [12 * NTOK, 128], BF16)
        ka0 = C.scratch("ka0", [12 * 2 * 128, NTOK], BF16)
        va0 = C.scratch("va0", [12 * 2 * NTOK, 128], BF16)
        kx1 = C.scratch("kx1", [5 * 128, NTOK], BF16)
        vx1 = C.scratch("vx1", [4 * NTOK, 128], BF16)
        ka1 = C.scratch("ka1", [5 * 2 * 128, NTOK], BF16)
        va1 = C.scratch("va1", [4 * 2 * NTOK, 128], BF16)

        def exchange(kx, ka, nk, vx, va, nv):
            prev = Tok()
            for h in range(nk):
                t1 = Tok()
                allgather(C, kx[h * 128:(h + 1) * 128, :], ka[h * 256:(h + 1) * 256, :], PAIRS, prev, t1)
                prev = t1
            for h in range(nv):
                t1 = Tok()
                allgather(C, vx[h * NTOK:(h + 1) * NTOK, :], va[h * 2 * NTOK:(h + 1) * 2 * NTOK, :], PAIRS, prev, t1)
                prev = t1
            S.barrier()
        iqT_s = C.scratch("iqT_s", [8, 128, NTOK], BF16)
        iw_s = C.scratch("iw_s", [NTOK, 16], F32)
        gcl = lambda n, l: C.par[:, PAR[(n, l)]:PAR[(n, l)] + 16]
        with ExitStack() as lay:
            kmv = mem_kv(C, lay, mem, W["mem_w_kv_0"], 0)
            xT = C.sb(lay, "xT", [128, DC, NT], F32)
            for tt in range(NTOK // NT):
                t_x = [toks(NT // 512) for _ in range(DC)]
                col0 = tt * NT
                load_x_tokmajor(C, x[col0:col0 + NT, :], xT, t_x, NT)
                ffn(C, xT, t_x, gcl("ffn1_g", 0), W["ffn1_w_gate_0"], W["ffn1_w_up_0"], W["ffn1_w_down_0"], NT)
                in_proj_A(C, xT, t_x, col0, W["a_w_in_0"], kmv, qT_s,
                          kx0.rearrange("(h p) t -> h p t", p=128), vx0.rearrange("(h t) c -> h t c", h=12), yT_s[12:16])
                store_xT(C, xT_s, xT, t_x, NT, col0)
                S.barrier()
        for layer in range(nlayers):
            t_ka, t_va, t0 = Tok(), Tok(), Tok()
            if stop == 1:
                break
            if layer == 0:
                exchange(kx0, ka0, 12, vx0, va0, 12)
                if stop == 2:
                    break
                diff_attention(C, qT_s, ka0.rearrange("(h r p) t -> r h p t", r=2, p=128),
                               va0.rearrange("(h r t) c -> h r t c", h=12, r=2), yT_s)
            else:
                exchange(kx1, ka1, 5, vx1, va1, 4)
                dsa_attention(C, qT_s, iqT_s, iw_s, ka1.rearrange("(h r p) t -> r h p t", r=2, p=128),
                              va1.rearrange("(h r t) c -> h r t c", h=4, r=2), cmt, yT_s)
            if stop == 3:
                break
            with ExitStack() as lay:
                last = layer == nlayers - 1
                if not last:
                    kmv = mem_kv(C, lay, mem, W[f"mem_w_kv_{layer + 1}"], layer + 1)
                xT = C.sb(lay, "xT", [128, DC, NT], F32)
                for tt in range(NTOK // NT):
                    t_x = [toks(NT // 512) for _ in range(DC)]
                    col0 = tt * NT
                    load_xT(C, xT_s, xT, t_x, NT, col0)
                    out_proj(C, xT, t_x, yT_s, col0, W[f"w_out_{layer}"])
                    ffn(C, xT, t_x, gcl("ffn2_g", layer), W[f"ffn2_w_gate_{layer}"], W[f"ffn2_w_up_{layer}"],
                        W[f"ffn2_w_down_{layer}"], NT)
                    if last:
                        store_x_tokmajor(C, xT, t_x, y[col0:col0 + NT, :], NT)
                    else:
                        ffn(C, xT, t_x, gcl("ffn1_g", layer + 1), W[f"ffn1_w_gate_{layer + 1}"], W[f"ffn1_w_up_{layer + 1}"],
                            W[f"ffn1_w_down_{layer + 1}"], NT)
                        in_proj_B(C, xT, t_x, col0, W["b_w_in_0"], kmv, qT_s,
                                  kx1.rearrange("(h p) t -> h p t", p=128), vx1.rearrange("(h t) c -> h t c", h=4), iqT_s, iw_s, yT_s[12:16])
                        store_xT(C, xT_s, xT, t_x, NT, col0)
                    S.barrier()
        print("program", finish(C))
    return nc


def bucket_np(n):
    n = np.maximum(n, 0)
    nf = np.maximum(n, 1).astype(np.float32)
    large = 16 + (np.log(nf / 16) / math.log(128 / 16) * 16).astype(np.int32)
    large = np.minimum(large, 31)
    return np.where(n < 16, n, large)


def window_tables(rel_bias, hf):
    s = np.arange(128)[:, None]
    t = np.arange(128)[None, :]
    tabs = {}
    d = t - s
    tabs["diag"] = np.where((d >= 0)[None], rel_bias[bucket_np(d)].transpose(2, 0, 1), NEG)
    tabs["off"] = rel_bias[bucket_np(128 + d)].transpose(2, 0, 1)
    tabs["far"] = np.broadcast_to(rel_bias[31][:, None, None], (12, 128, 128))
    tabs["masked"] = np.full((12, 128, 128), NEG, np.float32)
    order = ("off", "diag", "masked") if hf == 0 else ("far", "off", "diag")
    out = np.stack([tabs[o] for o in order], 0)
    return np.ascontiguousarray(out.transpose(2, 0, 1, 3)).astype(np.float32)


_NLAYERS = 2


def kernel(**inp):
    inp = {k: np.asarray(v) for k, v in inp.items()}
    nc = build_program(_NLAYERS)
    cst, par = const_array(), pack_params(inp)
    maps = []
    for c in range(8):
        b, hf = c // 2, c % 2
        m = {"x": own_rows(inp["x"][b], hf), "mem": np.ascontiguousarray(inp["mem"][b]), "cst": cst, "par": par,
             "wtab": window_tables(inp["rel_bias"], hf), "cmt": causal_tables(hf)}
        for k, i, rows, cols in WEIGHTS:
            if i >= _NLAYERS or (k == "b_w_in" and _NLAYERS < 2):
                continue
            r = rows // 8
            m[wname(k, i)] = np.ascontiguousarray(inp[k][i][c * r:(c + 1) * r])
        maps.append(m)
    res = run_bass_kernel_spmd(nc, maps, core_ids=list(range(8))).results
    out = np.zeros((4, T, D), np.float32)
    for c in range(8):
        b, hf = c // 2, c % 2
        out[b].reshape(16, 2, 128, D)[:, hf] = res[c]["y"].reshape(16, 128, D)
    return out


def in_proj_B(C, xT, t_x, col0, w_in, kmv, qT_s, kx1, vx1, iqT_s, iw_s, ymT_o):
    S = C.S
    kmT, vm, t_km, t_vm = kmv
    nh = NT // 512
    with ExitStack() as ph:
        hT = C.sb(ph, "ihT", [128, DC, NT], BF16)
        t_h = toks(nh)
        g0 = PAR[("mix_g", 1)]
        norm_to_hT(C, ph, xT, t_x, C.par[:, g0:g0 + 16], hT, t_h, NT, 7)
        wt = [C.sb(ph, "iwt", [128, DC, 256], BF16) for _ in range(2)]
        t_wt = toks(2)
        R = NormRes(C, ph)
        stg = [C.sb(ph, "istg", [128, NT], BF16) for _ in range(2)]
        t_stg = toks(2)
        qm = [C.sb(ph, "iqm", [128, NT], BF16) for _ in range(4)]
        t_qm = toks(4)

        def flush(s, dst):
            ot = Tok()
            C.out_toks.append(ot)
            S.add("sp", lambda e: e.dma_start(out=dst, in_=stg[s][:]), reads=[t_stg[s]], writes=[ot], dma=True)

        def post(ci, hh, n, pp, tpp):
            cs = slice(hh * 512, hh * 512 + n)
            s = ci % 2
            if ci < 16:
                gcol = C.dcol("b_gq_s") if ci < 12 else C.col("b_gk")
                post_norm(C, R, pp, tpp, n, C.ones128, gcol, stg[s][:, cs], t_stg[s])
                if hh == nh - 1:
                    flush(s, qT_s[ci, :, col0:col0 + NT] if ci < 12 else kx1[ci - 12, :, col0:col0 + NT])
            elif ci < 24:
                S.add("act", lambda e: e.activation(out=stg[s][:, cs], in_=pp, func=AF.Copy), reads=[tpp], writes=[t_stg[s]])
                if hh == nh - 1:
                    flush(s, iqT_s[ci - 16, :, col0:col0 + NT])
            else:
                hm = ci - 24
                post_norm(C, R, pp, tpp, n, C.ones128, C.dcol("mem_gq_s1"), qm[hm][:, cs], t_qm[hm])

        groups = ([(c0, 256) for c0 in range(0, 2048, 256)] + [(c0, 256) for c0 in range(2560, 3584, 256)]
                  + [(3664, 256), (3920, 256)])
        proj_fm(C, wt, t_wt, w_in, groups, hT, t_h, NT, post)

        for half in range(2):
            load_w(C, "pool", wt[0][:, :, half * 64:(half + 1) * 64], w_in, 3584, 64, t_wt[0])
        for hh in range(nh):
            cs = slice(hh * 512, (hh + 1) * 512)
            pp, tpp = C.psum[hh % 2], C.ptok[hh % 2]
            for k in range(DC):
                S.add("pe", lambda e, k=k, pp=pp, cs=cs: e.matmul(pp[:], lhsT=wt[0][:, k, 0:128], rhs=hT[:, k, cs],
                                                                  start=(k == 0), stop=(k == DC - 1)),
                      reads=[t_wt[0], t_h[hh]], writes=[tpp])
            S.add("act", lambda e, pp=pp, cs=cs: e.activation(out=stg[0][:, cs], in_=pp[:], func=AF.Copy), reads=[tpp], writes=[t_stg[0]])
        flush(0, kx1[4, :, col0:col0 + NT])

        wv = [C.sb(ph, "iwv", [128, DC, 512], BF16) for _ in range(2)]
        t_wv = toks(2)
        vst = [C.sb(ph, "ivst", [128, 512], BF16) for _ in range(2)]
        t_vst = toks(2)
        iwst = [C.sb(ph, "iwst", [128, 16], F32) for _ in range(2)]
        t_iwst = toks(2)
        vc = [0]

        def post_v(g, tt, w, pp, tpp):
            s = vc[0] % 2
            vc[0] += 1
            S.add("act", lambda e: e.activation(out=vst[s][:], in_=pp, func=AF.Copy), reads=[tpp], writes=[t_vst[s]])
            ot = Tok()
            C.out_toks.append(ot)
            r0 = col0 + tt * 128
            S.add("sp", lambda e: e.dma_start(out=vx1[:, r0:r0 + 128, :].rearrange("h t c -> t h c"),
                                              in_=vst[s][:].rearrange("t (h c) -> t h c", h=4)),
                  reads=[t_vst[s]], writes=[ot], dma=True)
        proj_tm(C, wv, t_wv, w_in, [(2048, 512)], hT, t_h, NT, post_v)

        def post_iw(g, tt, w, pp, tpp):
            s = vc[0] % 2
            vc[0] += 1
            S.add("dve", lambda e: e.tensor_scalar(out=iwst[s][:], in0=pp, scalar1=0.25 * 0.125, scalar2=None, op0=ALU.mult),
                  reads=[tpp], writes=[t_iwst[s]])
            ot = Tok()
            C.out_toks.append(ot)
            r0 = col0 + tt * 128
            S.add("sp", lambda e: e.dma_start(out=iw_s[r0:r0 + 128, :], in_=iwst[s][:]), reads=[t_iwst[s]], writes=[ot], dma=True)
        proj_tm(C, wv, t_wv, w_in, [(3648, 16)], hT, t_h, NT, post_iw)

        ym = [C.sb(ph, "iym", [128, NT], BF16) for _ in range(4)]
        t_ym = toks(4)
        mem_attn(C, ph, qm, t_qm, kmT, vm, t_km, t_vm, NT, lambda hm, cs: ym[hm][:, cs], lambda hm, hh: t_ym[hm])
        for hm in range(4):
            ot = Tok()
            C.out_toks.append(ot)
            S.add("sp", lambda e, hm=hm: e.dma_start(out=ymT_o[hm, :, col0:col0 + NT], in_=ym[hm][:]),
                  reads=[t_ym[hm]], writes=[ot], dma=True)
    S.barrier()


NEGBIG = -1.0e30


def dsa_attention(C, qT_s, iqT_s, iw_s, ka1, va1, cm_dram, yT_s):
    S = C.S
    with ExitStack() as ph:
        kseq = [C.sb(ph, "kseq", [128, T], BF16) for _ in range(4)]
        t_kseq = toks(4)
        for g in range(4):
            load_seq(C, "sp", kseq[g][:], ka1[:, g], t_kseq[g])
        ikseq = C.sb(ph, "ikseq", [128, T], BF16)
        t_ik = Tok()
        load_seq(C, "sp", ikseq[:], ka1[:, 4], t_ik)
        vseq = C.sb(ph, "vseq", [128, 32, 512], BF16)
        t_vs = Tok()
        for g in range(4):
            load_vseq(C, "sp", vseq[:, :, g * 128:(g + 1) * 128], lambda r, g=g: va1[g, r], t_vs)
        cm = C.sb(ph, "cm", [128, 3, 128], F32)
        t_cm = Tok()
        S.add("sp", lambda e: e.dma_start(out=cm[:], in_=cm_dram), writes=[t_cm], dma=True)
        iwt = C.sb(ph, "iwt", [128, NLT, 16], F32)
        t_iw = Tok()
        S.add("sp", lambda e: e.dma_start(out=iwt[:], in_=iw_s.rearrange("(l p) j -> p l j", p=128)), writes=[t_iw], dma=True)
        acc = C.sb(ph, "acc", [128, T], F32)
        work = C.sb(ph, "work", [128, T], F32)
        m8 = C.sb(ph, "m8", [128, 8], F32)
        thr = C.sb(ph, "thr", [128, 1], F32)
        t_acc, t_work, t_m8, t_thr = Tok(), Tok(), Tok(), Tok()
        mb = [C.sb(ph, "mb", [128, T], BF16) for _ in range(4)]
        t_mb = toks(4)
        rl = [C.sb(ph, "rl", [128, 512], F32) for _ in range(2)]
        t_rl = toks(2)
        iqh = [C.sb(ph, "iqh", [128, 8, 128], BF16) for _ in range(2)]
        t_iqh = toks(2)
        qh = [C.sb(ph, "dqh", [128, 512], BF16) for _ in range(2)]
        t_qh = toks(2)
        E = [C.sb(ph, "dE", [128, 512], BF16) for _ in range(3)]
        t_E = toks(3)
        rr = C.sb(ph, "drr", [128, 512], F32)
        t_rr = Tok()
        yst = [C.sb(ph, "dyst", [128, 512], BF16) for _ in range(2)]
        t_yst = toks(2)
        ca = cb = cq = cy = 0
        for ci in range(4):
            for i in range(4):
                lq = 4 * ci + i
                L = (2 * lq + 2) * 128
                si = lq % 2
                S.add("sp", lambda e, si=si, lq=lq: e.dma_start(
                    out=iqh[si][:], in_=iqT_s[:, :, lq * 128:(lq + 1) * 128].rearrange("c p t -> p c t")),
                    writes=[t_iqh[si]], dma=True)
                for kb in range((L + 511) // 512):
                    nk = min(512, L - kb * 512)
                    ks = slice(kb * 512, kb * 512 + nk)
                    for j in range(16):
                        c, half = j // 2, j % 2
                        hs = slice(half * 64, half * 64 + 64)
                        pb = ca % 2
                        ca += 1
                        ps, tps = C.psum[pb], C.ptok[pb]
                        S.add("pe", lambda e, ps=ps, hs=hs, c=c, si=si, ks=ks, nk=nk: e.matmul(
                            ps[:, :nk], lhsT=iqh[si][hs, c, :], rhs=ikseq[hs, ks], start=True, stop=True),
                            reads=[t_iqh[si], t_ik], writes=[tps])
                        S.add("act", lambda e, ps=ps, pb=pb, nk=nk: e.activation(out=rl[pb][:, :nk], in_=ps[:, :nk], func=AF.Relu),
                              reads=[tps], writes=[t_rl[pb]])
                        wcol = iwt[:, lq, j:j + 1]
                        if j == 0:
                            S.add("dve", lambda e, pb=pb, ks=ks, nk=nk, wcol=wcol: e.tensor_scalar(
                                out=acc[:, ks], in0=rl[pb][:, :nk], scalar1=wcol, scalar2=None, op0=ALU.mult),
                                reads=[t_rl[pb], t_iw], writes=[t_acc])
                        else:
                            S.add("dve", lambda e, pb=pb, ks=ks, nk=nk, wcol=wcol: e.scalar_tensor_tensor(
                                out=acc[:, ks], in0=rl[pb][:, :nk], scalar=wcol, in1=acc[:, ks], op0=ALU.mult, op1=ALU.add),
                                reads=[t_rl[pb], t_iw, t_acc], writes=[t_acc])
                for w in range(3):
                    jk = 2 * lq - 1 + w
                    if jk < 0:
                        continue
                    S.add("dve", lambda e, w=w, jk=jk: e.tensor_tensor(
                        out=acc[:, jk * 128:(jk + 1) * 128], in0=acc[:, jk * 128:(jk + 1) * 128], in1=cm[:, w, :], op=ALU.add),
                        reads=[t_acc, t_cm], writes=[t_acc])
                S.add("act", lambda e, L=L: e.activation(out=work[:, :L], in_=acc[:, :L], func=AF.Copy), reads=[t_acc], writes=[t_work])
                for r in range(32):
                    S.add("dve", lambda e, L=L: e.max(out=m8[:], in_=work[:, :L]), reads=[t_work], writes=[t_m8])
                    if r < 31:
                        S.add("dve", lambda e, L=L: e.match_replace(out=work[:, :L], in_to_replace=m8[:], in_values=work[:, :L],
                                                                    imm_value=NEGBIG), reads=[t_m8, t_work], writes=[t_work])
                S.add("dve", lambda e: e.tensor_scalar(out=thr[:], in0=m8[:, 7:8], scalar1=-1.0e29, scalar2=None, op0=ALU.max),
                      reads=[t_m8], writes=[t_thr])
                S.add("dve", lambda e, L=L: e.tensor_scalar(out=work[:, :L], in0=acc[:, :L], scalar1=thr[:, 0:1], scalar2=None, op0=ALU.is_ge),
                      reads=[t_acc, t_thr], writes=[t_work])
                S.add("pool", lambda e, i=i, L=L: e.tensor_scalar(out=mb[i][:, :L], in0=work[:, :L], scalar1=-1.0, scalar2=-NEG,
                                                                 op0=ALU.add, op1=ALU.mult),
                      reads=[t_work], writes=[t_mb[i]])
            q0 = ci * 512
            keys = list(chunk_keys(ci))
            for h in range(12):
                g = h // 3
                sq_ = cq % 2
                cq += 1
                S.add("sp", lambda e, sq_=sq_, h=h, q0=q0: e.dma_start(out=qh[sq_][:], in_=qT_s[h, :, q0:q0 + 512]), writes=[t_qh[sq_]], dma=True)
                pnum, tnum = C.psum[5], C.ptok[5]
                pden, tden = C.psum[6], C.ptok[6]
                for jk, istart in keys:
                    c0 = istart * 128
                    n = 512 - c0
                    pb = 2 + cb % 3
                    es = cb % 3
                    cb += 1
                    ps, tps = C.psum[pb], C.ptok[pb]
                    S.add("pe", lambda e, ps=ps, g=g, jk=jk, c0=c0, sq_=sq_: e.matmul(
                        ps[:, c0:512], lhsT=kseq[g][:, jk * 128:(jk + 1) * 128], rhs=qh[sq_][:, c0:512],
                        start=True, stop=False, skip_group_check=True),
                        reads=[t_kseq[g], t_qh[sq_]], writes=[tps])
                    for i in range(istart, 4):
                        S.add("pe", lambda e, ps=ps, i=i, jk=jk: e.matmul(
                            ps[:, i * 128:(i + 1) * 128], lhsT=mb[i][:, jk * 128:(jk + 1) * 128], rhs=C.ident,
                            start=False, stop=False, skip_group_check=True),
                            reads=[t_mb[i], C.t_c], writes=[tps])
                    window_adds(C, ps, tps, jk, ci, h, istart)
                    Es, tE = E[es], t_E[es]
                    S.add("act", lambda e, ps=ps, Es=Es, c0=c0, n=n, h=h: e.activation(
                        out=Es[:, :n], in_=ps[:, c0:512], func=AF.Exp,
                        bias=C.par[:, PAR["b31"] + h:PAR["b31"] + h + 1], scale=1.0),
                        reads=[tps, C.t_par], writes=[tE])
                    first, last = (jk == 0), (jk == keys[-1][0])
                    S.add("pe", lambda e, Es=Es, jk=jk, g=g, c0=c0, n=n, first=first, last=last: e.matmul(
                        pnum[:, c0:512], lhsT=vseq[:, jk, g * 128:(g + 1) * 128], rhs=Es[:, :n], start=first, stop=last,
                        skip_group_check=True), reads=[t_vs, tE], writes=[tnum])
                    S.add("pe", lambda e, Es=Es, c0=c0, n=n, first=first, last=last: e.matmul(
                        pden[:, c0:512], lhsT=C.ones1, rhs=Es[:, :n], start=first, stop=last, skip_group_check=True),
                        reads=[tE, C.t_c], writes=[tden])
                S.add("dve", lambda e: e.reciprocal(out=rr[:], in_=pden[:]), reads=[tden], writes=[t_rr])
                ys = cy % 2
                cy += 1
                S.add("dve", lambda e, ys=ys: e.tensor_tensor(out=yst[ys][:], in0=pnum[:], in1=rr[:], op=ALU.mult),
                      reads=[tnum, t_rr], writes=[t_yst[ys]])
                ot = Tok()
                C.out_toks.append(ot)
                S.add("sp", lambda e, ys=ys, h=h, q0=q0: e.dma_start(out=yT_s[h, :, q0:q0 + 512], in_=yst[ys][:]),
                      reads=[t_yst[ys]], writes=[ot], dma=True)
    S.barrier()


def causal_tables(hf):
    t = np.arange(128)[:, None]
    s = np.arange(128)[None, :]
    diag = np.where(s <= t, 0.0, NEGBIG).astype(np.float32)
    zero = np.zeros((128, 128), np.float32)
    allm = np.full((128, 128), NEGBIG, np.float32)
    order = (zero, diag, allm) if hf == 0 else (zero, zero, diag)
    return np.ascontiguousarray(np.stack(order, 1))
```

```python
import math
from contextlib import ExitStack

import numpy as np
import ml_dtypes
import concourse.bass as bass
import concourse.mybir as mybir
from concourse.bass_utils import run_bass_kernel_spmd

F32 = mybir.dt.float32
BF16 = mybir.dt.bfloat16
AF = mybir.ActivationFunctionType
ALU = mybir.AluOpType

D = 2048
DC = 16
FF = 5632
T = 4096
NTOK = 2048
NT = 1024
NLT = NTOK // 128
EPS = 1e-6
NEG = -30000.0


class Tok:
    __slots__ = ("name", "w", "r")

    def __init__(self, name=""):
        self.name = name
        self.w = None
        self.r = []


def toks(n):
    return [Tok() for _ in range(n)]


class Op:
    __slots__ = ("eng", "fn", "reads", "writes", "deps", "sig", "sidx", "dma",
                 "dsem", "dval", "id", "eidx", "xdeps", "cc")


COMPUTE = ("pe", "act", "dve", "pool")


class Sched:
    def __init__(self, nc):
        self.nc = nc
        self.ops = []
        self.eng_obj = {"pe": nc.tensor, "act": nc.scalar, "dve": nc.vector,
                        "pool": nc.gpsimd, "sp": nc.sync}
        self.n_dma_sems = {"sp": 8, "act": 4, "pool": 8}
        self.ecount = {e: 0 for e in self.eng_obj}
        self.last = {e: None for e in self.eng_obj}
        self.dma_since = []
        self.pending = {}

    def add(self, eng, fn, reads=(), writes=(), dma=False, cc=False):
        op = Op()
        op.cc = cc
        op.eng = eng
        op.fn = fn
        op.reads = list(reads)
        op.writes = list(writes)
        op.dma = dma
        op.sig = False
        op.sidx = 0
        op.deps = None
        op.dsem = None
        op.dval = 0
        op.id = len(self.ops)
        op.eidx = self.ecount[eng]
        self.ecount[eng] += 1
        op.xdeps = self.pending.pop(eng, ())
        self.ops.append(op)
        if dma:
            self.dma_since.append(op)
        else:
            self.last[eng] = op
        return op

    def barrier(self):
        deps = [o for o in self.last.values() if o is not None] + list(self.dma_since)
        for e in self.eng_obj:
            self.pending[e] = tuple(self.pending.get(e, ())) + tuple(deps)
        self.dma_since = []

    def emit(self, stack):
        nc = self.nc
        for op in self.ops:
            deps = {}
            for t in op.reads:
                if t.w is not None:
                    deps[t.w.id] = (t.w, True)
            for t in op.writes:
                if t.w is not None and t.w.id not in deps:
                    deps[t.w.id] = (t.w, False)
                lastr = {}
                for r in t.r:
                    lastr[(r.eng, r.dma)] = r
                    if r.dma and r.id not in deps:
                        deps[r.id] = (r, False)
                for r in lastr.values():
                    if r.id not in deps:
                        deps[r.id] = (r, False)
            need = []
            for d, raw in deps.values():
                if d is op:
                    continue
                if (not d.dma) and (not op.dma) and d.eng == op.eng:
                    if d.eng == "pe" or not raw:
                        continue
                    if d.eng in ("act", "dve") and op.eidx - d.eidx > 3:
                        continue
                need.append(d)
            for d in op.xdeps:
                if d is not op:
                    need.append(d)
            for d in need:
                if not d.dma:
                    d.sig = True
            op.deps = need
            for t in op.reads:
                t.r.append(op)
            for t in op.writes:
                t.w = op
                t.r = []
        esem = {e: stack.enter_context(nc.semaphore("s_" + e)) for e in COMPUTE}
        dsems = {q: [stack.enter_context(nc.semaphore(f"d_{q}{i}")) for i in range(n)]
                 for q, n in self.n_dma_sems.items()}
        dcnt = {q: [0] * n for q, n in self.n_dma_sems.items()}
        dnext = {q: 0 for q in self.n_dma_sems}
        ecnt = {e: 0 for e in COMPUTE}
        known = {e: {} for e in self.eng_obj}
        nwait = 0
        for op in self.ops:
            eo = self.eng_obj[op.eng]
            kn = known[op.eng]
            waits = {}
            for d in op.deps:
                if d.dma:
                    key, val = d.dsem, d.dval
                else:
                    key, val = esem[d.eng], d.sidx
                nm = id(key)
                if kn.get(nm, 0) >= val:
                    continue
                if nm not in waits or waits[nm][1] < val:
                    waits[nm] = (key, val)
            if op.cc:
                op.dsem = stack.enter_context(nc.semaphore(f"cc{op.id}"))
                op.dval = 1
            elif op.dma:
                q = op.eng
                i = dnext[q]
                dnext[q] = (i + 1) % len(dsems[q])
                sem = dsems[q][i]
                prev = dcnt[q][i] * 16
                dcnt[q][i] += 1
                op.dsem = sem
                op.dval = dcnt[q][i] * 16
                nm = id(sem)
                if prev > 0 and kn.get(nm, 0) < prev:
                    if nm not in waits or waits[nm][1] < prev:
                        waits[nm] = (sem, prev)
            for nm, (key, val) in waits.items():
                eo.wait_ge(key, val)
                kn[nm] = val
                nwait += 1
            if op.fn is None:
                continue
            inst = op.fn(eo)
            if op.cc:
                inst.then_inc(op.dsem)
            elif op.dma:
                inst.then_inc(op.dsem, 16)
            elif op.sig:
                ecnt[op.eng] += 1
                op.sidx = ecnt[op.eng]
                inst.then_inc(esem[op.eng], 1)
        self.stats = dict(n_ops=len(self.ops), n_waits=nwait, sig=dict(ecnt))
        return self.stats


class Ctx:
    def __init__(self, nc, st):
        self.nc = nc
        self.st = st
        self.S = Sched(nc)
        self.ins = {}
        self.outs = {}
        self.out_toks = []
        self.scr_toks = {}
        self.uid = 0
        self.psum = [st.enter_context(nc.psum_tensor(f"ps{i}", [128, 512], F32)) for i in range(8)]
        self.ptok = toks(8)

    def name(self, p):
        self.uid += 1
        return f"{p}{self.uid}"

    def inp(self, name, shape, dtype=F32):
        t = self.nc.dram_tensor(name, list(shape), dtype, kind="ExternalInput").ap()
        self.ins[name] = t
        return t

    def out(self, name, shape, dtype=F32):
        t = self.nc.dram_tensor(name, list(shape), dtype, kind="ExternalOutput").ap()
        self.outs[name] = t
        return t

    def scratch(self, name, shape, dtype):
        return self.nc.dram_tensor(name, list(shape), dtype, kind="Internal").ap()

    def sb(self, stack, name, shape, dtype):
        return stack.enter_context(self.nc.sbuf_tensor(self.name(name), list(shape), dtype))

    def consts(self, cst):
        S = self.S
        st = self.st
        c32 = self.sb(st, "c32", [128, 640], F32)
        self.cbf = self.sb(st, "cbf", [128, 640], BF16)
        self.t_c = Tok()
        t32 = Tok()
        S.add("sp", lambda e: e.dma_start(out=c32[:], in_=cst), writes=[t32], dma=True)
        S.add("dve", lambda e: e.tensor_copy(out=self.cbf[:], in_=c32[:]), reads=[t32], writes=[self.t_c])
        self.ident32 = c32[:, 0:128]
        self.t_c32 = t32
        self.ident = self.cbf[:, 0:128]
        self.onesD = self.cbf[:, 128:256]
        self.blk64 = self.cbf[:, 256:384]
        self.ones128 = self.cbf[:, 384:512]
        self.ones1 = self.cbf[:, 512:640]


def const_array():
    c = np.zeros((128, 640), np.float32)
    c[:, 0:128] = np.eye(128)
    c[:, 128:256] = 1.0 / D
    c[0:64, 256:320] = 1.0 / 64
    c[64:128, 320:384] = 1.0 / 64
    c[:, 384:512] = 1.0 / 128
    c[:, 512:640] = 1.0
    return c


class WT_:
    def __init__(self, ap, tok):
        self.ap = ap
        self.tok = tok


def load_w(C, eng, dst, W, c0, w, t_dst):
    C.S.add(eng, lambda e: e.dma_start(out=dst, in_=W.ap[:, c0:c0 + w].rearrange("(c p) f -> p c f", p=128)),
            reads=[W.tok], writes=[t_dst], dma=True)


def norm_to_hT(C, ph, xT, t_x, gcol, hT, t_h, ntok, pb):
    S = C.S
    nh = (ntok + 511) // 512
    ph = ExitStack()
    sq = C.sb(ph, "sq", [128, DC, 512], BF16)
    rstd = [C.sb(ph, "rstd", [128, 512], F32) for _ in range(2)]
    t_sq = Tok()
    t_r = toks(2)
    pn = C.psum[pb]
    t_pn = C.ptok[pb]
    for hh in range(nh):
        n = min(512, ntok - hh * 512)
        cs = slice(hh * 512, hh * 512 + n)
        S.add("act", lambda e, cs=cs, n=n: e.activation(out=sq[:, :, :n], in_=xT[:, :, cs], func=AF.Square),
              reads=[t_x[k][hh] for k in range(DC)], writes=[t_sq])
        for k in range(DC):
            S.add("pe", lambda e, k=k, n=n: e.matmul(pn[:, :n], lhsT=C.onesD, rhs=sq[:, k, :n],
                                                     start=(k == 0), stop=(k == DC - 1)),
                  reads=[t_sq, C.t_c], writes=[t_pn])
        r = rstd[hh % 2]
        tr = t_r[hh % 2]
        S.add("act", lambda e, r=r, n=n: e.activation(out=r[:, :n], in_=pn[:, :n], func=AF.Sqrt, bias=EPS, scale=1.0),
              reads=[t_pn], writes=[tr])
        S.add("dve", lambda e, r=r, n=n: e.reciprocal(out=r[:, :n], in_=r[:, :n]), reads=[tr], writes=[tr])
        for k in range(DC):
            S.add("dve", lambda e, k=k, cs=cs, r=r, n=n: e.scalar_tensor_tensor(
                out=hT[:, k, cs], in0=xT[:, k, cs], scalar=gcol[:, k:k + 1], in1=r[:, :n],
                op0=ALU.mult, op1=ALU.mult),
                reads=[t_x[k][hh], tr, C.t_par], writes=[t_h[hh]])
    ph.close()
    S.barrier()


def ffn(C, xT, t_x, gcol, wg, wu, wd, ntok):
    S = C.S
    nh = ntok // 512
    with ExitStack() as ph:
        hT = C.sb(ph, "hT", [128, DC, ntok], BF16)
        t_h = toks(nh)
        norm_to_hT(C, ph, xT, t_x, gcol, hT, t_h, ntok, 6)
        wgt = [C.sb(ph, "wg", [128, DC, 256], BF16) for _ in range(2)]
        wut = [C.sb(ph, "wu", [128, DC, 256], BF16) for _ in range(2)]
        wdt = [C.sb(ph, "wd", [128, 2, D], BF16) for _ in range(2)]
        sg = [C.sb(ph, "sg", [128, 512], F32) for _ in range(2)]
        hid = [C.sb(ph, "hid", [128, 2, ntok], BF16) for _ in range(2)]
        t_wg, t_wu, t_wd, t_sg = toks(2), toks(2), toks(2), toks(2)
        dtmp = [C.sb(ph, "dtmp", [128, 512], F32) for _ in range(2)]
        t_dtmp = toks(2)
        t_hid = [[toks(nh) for _ in range(2)] for _ in range(2)]
        NFG = FF // 256

        def load_gu(fg):
            if fg >= NFG:
                return
            s = fg % 2
            f0 = fg * 256
            load_w(C, "pool", wgt[s][:], wg, f0, 256, t_wg[s])
            load_w(C, "pool", wut[s][:], wu, f0, 256, t_wu[s])

        def load_d(fg):
            if fg >= NFG:
                return
            s = fg % 2
            f0 = fg * 256
            S.add("pool", lambda e: e.dma_start(out=wdt[s][:], in_=wd.ap[f0:f0 + 256, :].rearrange("(c p) d -> p c d", p=128)),
                  reads=[wd.tok], writes=[t_wd[s]], dma=True)

        cnt = [0]

        def gu(fg):
            s = fg % 2
            for hh in range(nh):
                cs = slice(hh * 512, (hh + 1) * 512)
                for c in range(2):
                    ps = cnt[0] % 2
                    cnt[0] += 1
                    pg, pu = C.psum[ps], C.psum[2 + ps]
                    tpg, tpu = C.ptok[ps], C.ptok[2 + ps]
                    for k in range(DC):
                        S.add("pe", lambda e, k=k, pg=pg, c=c, cs=cs: e.matmul(
                            pg[:], lhsT=wgt[s][:, k, c * 128:(c + 1) * 128], rhs=hT[:, k, cs],
                            start=(k == 0), stop=(k == DC - 1)),
                            reads=[t_wg[s], t_h[hh]], writes=[tpg])
                    for k in range(DC):
                        S.add("pe", lambda e, k=k, pu=pu, c=c, cs=cs: e.matmul(
                            pu[:], lhsT=wut[s][:, k, c * 128:(c + 1) * 128], rhs=hT[:, k, cs],
                            start=(k == 0), stop=(k == DC - 1)),
                            reads=[t_wu[s], t_h[hh]], writes=[tpu])
                    S.add("act", lambda e, pg=pg, ps=ps: e.activation(out=sg[ps][:], in_=pg[:], func=AF.Silu),
                          reads=[tpg], writes=[t_sg[ps]])
                    S.add("dve", lambda e, pu=pu, ps=ps, c=c, cs=cs: e.tensor_tensor(
                        out=hid[s][:, c, cs], in0=sg[ps][:], in1=pu[:], op=ALU.mult),
                        reads=[t_sg[ps], tpu], writes=[t_hid[s][c][hh]])

        dcnt = [0]

        def down(fg):
            s = fg % 2
            for hh in range(nh):
                cs = slice(hh * 512, (hh + 1) * 512)
                for dc in range(DC):
                    ps = 4 + dcnt[0] % 2
                    dcnt[0] += 1
                    pd, tpd = C.psum[ps], C.ptok[ps]
                    for c in range(2):
                        S.add("pe", lambda e, c=c, pd=pd, dc=dc, cs=cs: e.matmul(
                            pd[:], lhsT=wdt[s][:, c, dc * 128:(dc + 1) * 128], rhs=hid[s][:, c, cs],
                            start=(c == 0), stop=(c == 1)),
                            reads=[t_wd[s], t_hid[s][c][hh]], writes=[tpd])
                    if dc % 2 == 0:
                        S.add("dve", lambda e, pd=pd, dc=dc, cs=cs: e.scalar_tensor_tensor(
                            out=xT[:, dc, cs], in0=pd[:], scalar=0.5, in1=xT[:, dc, cs], op0=ALU.mult, op1=ALU.add),
                            reads=[tpd, t_x[dc][hh]], writes=[t_x[dc][hh]])
                    else:
                        ts = (dcnt[0] // 2) % 2
                        S.add("act", lambda e, pd=pd, ts=ts: e.mul(dtmp[ts][:], pd[:], 0.5), reads=[tpd], writes=[t_dtmp[ts]])
                        S.add("pool", lambda e, dc=dc, cs=cs, ts=ts: e.tensor_tensor(
                            out=xT[:, dc, cs], in0=xT[:, dc, cs], in1=dtmp[ts][:], op=ALU.add),
                            reads=[t_dtmp[ts], t_x[dc][hh]], writes=[t_x[dc][hh]])

        load_gu(0)
        load_gu(1)
        load_d(0)
        for fg in range(NFG):
            gu(fg)
            if fg >= 1:
                down(fg - 1)
            load_gu(fg + 2)
            load_d(fg + 1)
        down(NFG - 1)
    S.barrier()


def load_x_tokmajor(C, x_dram, xT, t_x, ntok):
    S = C.S
    with ExitStack() as ph:
        xt = [C.sb(ph, "xt", [128, D], F32) for _ in range(2)]
        t_xt = toks(2)
        cnt = 0
        for tt in range(ntok // 128):
            s = tt % 2
            S.add("sp", lambda e, s=s, tt=tt: e.dma_start(out=xt[s][:], in_=x_dram[tt * 128:(tt + 1) * 128, :]),
                  writes=[t_xt[s]], dma=True)
            hh = tt // 4
            for g in range(4):
                pb = cnt % 2
                cnt += 1
                pt, tpt = C.psum[pb], C.ptok[pb]
                for j in range(4):
                    c = g * 4 + j
                    S.add("pe", lambda e, pt=pt, j=j, c=c, s=s: e.transpose(
                        pt[:, j * 128:(j + 1) * 128], xt[s][:, c * 128:(c + 1) * 128], C.ident32),
                        reads=[t_xt[s], C.t_c32], writes=[tpt])
                eng = "act" if g % 2 == 0 else "dve"
                if eng == "act":
                    fn = lambda e, pt=pt, g=g, tt=tt: e.activation(
                        out=xT[:, g * 4:(g + 1) * 4, tt * 128:(tt + 1) * 128],
                        in_=pt[:].rearrange("p (j t) -> p j t", j=4), func=AF.Copy)
                else:
                    fn = lambda e, pt=pt, g=g, tt=tt: e.tensor_copy(
                        out=xT[:, g * 4:(g + 1) * 4, tt * 128:(tt + 1) * 128],
                        in_=pt[:].rearrange("p (j t) -> p j t", j=4))
                S.add(eng, fn, reads=[tpt], writes=[t_x[c][hh] for c in range(g * 4, g * 4 + 4)])
    S.barrier()


def store_x_tokmajor(C, xT, t_x, y_dram, ntok):
    S = C.S
    with ExitStack() as ph:
        xt = [C.sb(ph, "xo", [128, D], F32) for _ in range(2)]
        t_xt = toks(2)
        cnt = 0
        for tt in range(ntok // 128):
            s = tt % 2
            hh = tt // 4
            for g in range(4):
                pb = cnt % 2
                cnt += 1
                pt, tpt = C.psum[pb], C.ptok[pb]
                for j in range(4):
                    c = g * 4 + j
                    S.add("pe", lambda e, pt=pt, j=j, c=c, tt=tt: e.transpose(
                        pt[:, j * 128:(j + 1) * 128], xT[:, c, tt * 128:(tt + 1) * 128], C.ident32),
                        reads=[t_x[c][hh], C.t_c32], writes=[tpt])
                eng = "act" if g % 2 == 0 else "dve"
                if eng == "act":
                    fn = lambda e, pt=pt, g=g, s=s: e.activation(out=xt[s][:, g * 512:(g + 1) * 512], in_=pt[:], func=AF.Copy)
                else:
                    fn = lambda e, pt=pt, g=g, s=s: e.tensor_copy(out=xt[s][:, g * 512:(g + 1) * 512], in_=pt[:])
                S.add(eng, fn, reads=[tpt], writes=[t_xt[s]])
            ot = Tok()
            C.out_toks.append(ot)
            S.add("sp", lambda e, s=s, tt=tt: e.dma_start(out=y_dram[tt * 128:(tt + 1) * 128, :], in_=xt[s][:]),
                  reads=[t_xt[s]], writes=[ot], dma=True)
    S.barrier()


def load_xT(C, src, xT, t_x, ntok, col0):
    for c in range(DC):
        C.S.add("sp", lambda e, c=c: e.dma_start(out=xT[:, c, :], in_=src[c, :, col0:col0 + ntok]),
                writes=[t_x[c][hh] for hh in range(ntok // 512)], dma=True)


def store_xT(C, dst, xT, t_x, ntok, col0):
    for c in range(DC):
        ot = Tok()
        C.out_toks.append(ot)
        C.S.add("sp", lambda e, c=c: e.dma_start(out=dst[c, :, col0:col0 + ntok], in_=xT[:, c, :]),
                reads=[t_x[c][hh] for hh in range(ntok // 512)], writes=[ot], dma=True)


def finish(C):
    C.S.add("sp", None, reads=C.out_toks)
    return C.S.emit(C.st)


PAR = {}
_o = 0
for _i in range(2):
    for _n in ("ffn1_g", "ffn2_g", "mix_g", "mem_g"):
        PAR[(_n, _i)] = _o
        _o += 16
    for _n in ("mem_gq", "mem_gk"):
        PAR[(_n, _i)] = _o
        _o += 1
for _n in ("a_gq2", "a_gk2", "a_gsub", "b_gq", "b_gk"):
    PAR[_n] = _o
    _o += 1
for _n in ("lq1", "lk1", "lq2", "lk2"):
    PAR[_n] = _o
    _o += 64
PAR["b31"] = _o
_o += 12
NPAR = _o
DPAR = {"a_gq2s": 0, "a_gsub_s": 1, "b_gq_s": 2, "neglam": 3, "mem_gq_s0": 4, "mem_gq_s1": 5, "tmp": 6}


def pack_params(inp):
    p = np.zeros((128, NPAR), np.float32)
    for i in range(2):
        for n in ("ffn1_g", "ffn2_g", "mix_g", "mem_g"):
            p[:, PAR[(n, i)]:PAR[(n, i)] + 16] = inp[n][i].reshape(16, 128).T
        p[:, PAR[("mem_gq", i)]] = inp["mem_gq"][i]
        p[:, PAR[("mem_gk", i)]] = inp["mem_gk"][i]
    p[:, PAR["a_gq2"]] = np.concatenate([inp["a_gq"][0], inp["a_gq"][0]])
    p[:, PAR["a_gk2"]] = np.concatenate([inp["a_gk"][0], inp["a_gk"][0]])
    p[:, PAR["a_gsub"]] = inp["a_g_sub"][0]
    p[:, PAR["b_gq"]] = inp["b_gq"][0]
    p[:, PAR["b_gk"]] = inp["b_gk"][0]
    for n, k in (("lq1", "a_lam_q1"), ("lk1", "a_lam_k1"), ("lq2", "a_lam_q2"), ("lk2", "a_lam_k2")):
        p[:, PAR[n]:PAR[n] + 64] = np.broadcast_to(inp[k][0][None, :], (128, 64))
    p[:, PAR["b31"]:PAR["b31"] + 12] = np.broadcast_to(inp["rel_bias"][31][None, :], (128, 12))
    return p


def load_params(C, par_dram):
    S = C.S
    st = C.st
    C.par = C.sb(st, "par", [128, NPAR], F32)
    C.dpar = C.sb(st, "dpar", [128, 8], F32)
    C.t_par = Tok()
    t0 = Tok()
    S.add("sp", lambda e: e.dma_start(out=C.par[:], in_=par_dram), writes=[t0], dma=True)
    P, Dp = C.par, C.dpar

    def col(n):
        return P[:, PAR[n]:PAR[n] + 1]

    def dcol(n):
        return Dp[:, DPAR[n]:DPAR[n] + 1]
    C.col = col
    C.dcol = dcol
    S.add("dve", lambda e: e.tensor_scalar(out=dcol("a_gq2s"), in0=col("a_gq2"), scalar1=0.125, scalar2=None, op0=ALU.mult), reads=[t0], writes=[C.t_par])
    S.add("dve", lambda e: e.tensor_scalar(out=dcol("a_gsub_s"), in0=col("a_gsub"), scalar1=0.8, scalar2=None, op0=ALU.mult), reads=[t0], writes=[C.t_par])
    sc = 128 ** -0.5
    S.add("dve", lambda e: e.tensor_scalar(out=dcol("b_gq_s"), in0=col("b_gq"), scalar1=sc, scalar2=None, op0=ALU.mult), reads=[t0], writes=[C.t_par])
    for i in range(2):
        S.add("dve", lambda e, i=i: e.tensor_scalar(out=dcol(f"mem_gq_s{i}"), in0=P[:, PAR[("mem_gq", i)]:PAR[("mem_gq", i)] + 1],
                                                    scalar1=sc, scalar2=None, op0=ALU.mult), reads=[t0], writes=[C.t_par])
    tmp = C.sb(st, "lamtmp", [128, 64], F32)
    s12 = C.sb(st, "lams", [128, 2], F32)
    tl = Tok()
    for j, (a, b) in enumerate((("lq1", "lk1"), ("lq2", "lk2"))):
        S.add("dve", lambda e, a=a, b=b: e.tensor_tensor(out=tmp[:], in0=P[:, PAR[a]:PAR[a] + 64], in1=P[:, PAR[b]:PAR[b] + 64], op=ALU.mult),
              reads=[t0], writes=[tl])
        S.add("dve", lambda e, j=j: e.reduce_sum(out=s12[:, j:j + 1], in_=tmp[:], axis=mybir.AxisListType.X), reads=[tl], writes=[tl])
    S.add("act", lambda e: e.activation(out=s12[:], in_=s12[:], func=AF.Exp), reads=[tl], writes=[tl])
    S.add("dve", lambda e: e.scalar_tensor_tensor(out=dcol("neglam"), in0=s12[:, 1:2], scalar=-0.2, in1=s12[:, 0:1], op0=ALU.add, op1=ALU.subtract),
          reads=[tl], writes=[C.t_par])


class NormRes:
    def __init__(self, C, ph, banks=(3, 4)):
        self.sq = [C.sb(ph, "nsq", [128, 512], BF16) for _ in range(2)]
        self.r = [C.sb(ph, "nr", [128, 512], F32) for _ in range(2)]
        self.t_sq = toks(2)
        self.t_r = toks(2)
        self.banks = banks
        self.i = 0


def post_norm(C, R, src, t_src, n, normmat, gcol, dst, t_dst, src_is_psum=True):
    S = C.S
    i = R.i
    R.i += 1
    s = i % 2
    pb = R.banks[i % len(R.banks)]
    pn, tpn = C.psum[pb], C.ptok[pb]
    sq, r = R.sq[s], R.r[s]
    S.add("act", lambda e: e.activation(out=sq[:, :n], in_=src, func=AF.Square), reads=[t_src], writes=[R.t_sq[s]])
    S.add("pe", lambda e: e.matmul(pn[:, :n], lhsT=normmat, rhs=sq[:, :n], start=True, stop=True),
          reads=[R.t_sq[s], C.t_c], writes=[tpn])
    S.add("act", lambda e: e.activation(out=r[:, :n], in_=pn[:, :n], func=AF.Sqrt, bias=EPS, scale=1.0), reads=[tpn], writes=[R.t_r[s]])
    S.add("dve", lambda e: e.reciprocal(out=r[:, :n], in_=r[:, :n]), reads=[R.t_r[s]], writes=[R.t_r[s]])
    S.add("dve", lambda e: e.scalar_tensor_tensor(out=dst, in0=src, scalar=gcol, in1=r[:, :n], op0=ALU.mult, op1=ALU.mult),
          reads=[t_src, R.t_r[s], C.t_par], writes=[t_dst])


def proj_fm(C, wt, t_wt, W, col_groups, hT, t_h, ntok, post, banks=(0, 1, 2)):
    S = C.S
    nh = (ntok + 511) // 512
    cnt = 0
    ci = 0

    def load(g):
        c0, w = col_groups[g]
        load_w(C, "pool", wt[g % 2][:, :, :w], W, c0, w, t_wt[g % 2])

    load(0)
    for g, (c0, w) in enumerate(col_groups):
        if g + 1 < len(col_groups):
            load(g + 1)
        s = g % 2
        for c in range(w // 128):
            for hh in range(nh):
                n = min(512, ntok - hh * 512)
                cs = slice(hh * 512, hh * 512 + n)
                pb = banks[cnt % len(banks)]
                cnt += 1
                pp, tpp = C.psum[pb], C.ptok[pb]
                for k in range(DC):
                    S.add("pe", lambda e, k=k, pp=pp, c=c, cs=cs, n=n, s=s: e.matmul(
                        pp[:, :n], lhsT=wt[s][:, k, c * 128:(c + 1) * 128], rhs=hT[:, k, cs],
                        start=(k == 0), stop=(k == DC - 1)),
                        reads=[t_wt[s], t_h[hh]], writes=[tpp])
                post(ci, hh, n, pp[:, :n], tpp)
            ci += 1


def proj_tm(C, wv, t_wv, W, col_groups, hT, t_h, ntok, post, banks=(5, 6)):
    S = C.S
    cnt = 0

    def load(g):
        c0, w = col_groups[g]
        load_w(C, "pool", wv[g % 2][:, :, :w], W, c0, w, t_wv[g % 2])

    load(0)
    for g, (c0, w) in enumerate(col_groups):
        if g + 1 < len(col_groups):
            load(g + 1)
        s = g % 2
        for tt in range(ntok // 128):
            pb = banks[cnt % len(banks)]
            cnt += 1
            pp, tpp = C.psum[pb], C.ptok[pb]
            hh = tt // 4
            for k in range(DC):
                S.add("pe", lambda e, k=k, pp=pp, tt=tt, s=s, w=w: e.matmul(
                    pp[:, :w], lhsT=hT[:, k, tt * 128:(tt + 1) * 128], rhs=wv[s][:, k, :w],
                    start=(k == 0), stop=(k == DC - 1)),
                    reads=[t_wv[s], t_h[hh]], writes=[tpp])
            post(g, tt, w, pp[:, :w], tpp)


def mem_kv(C, lay, mem_dram, wkv, layer):
    S = C.S
    kmT = [C.sb(lay, "kmT", [128, 256], BF16) for _ in range(4)]
    vm = [C.sb(lay, "vm", [128, 512], BF16) for _ in range(2)]
    t_km, t_vm = toks(4), toks(2)
    with ExitStack() as ph:
        mT = C.sb(ph, "mT", [128, DC, 256], F32)
        t_m = [toks(1) for _ in range(DC)]
        load_x_tokmajor(C, mem_dram, mT, t_m, 256)
        hT = C.sb(ph, "mhT", [128, DC, 256], BF16)
        t_h = toks(1)
        g0 = PAR[("mem_g", layer)]
        norm_to_hT(C, ph, mT, t_m, C.par[:, g0:g0 + 16], hT, t_h, 256, 7)
        wt = [C.sb(ph, "mwt", [128, DC, 256], BF16) for _ in range(2)]
        t_wt = toks(2)
        R = NormRes(C, ph)
        gk = C.par[:, PAR[("mem_gk", layer)]:PAR[("mem_gk", layer)] + 1]

        def post_k(ci, hh, n, pp, tpp):
            post_norm(C, R, pp, tpp, n, C.ones128, gk, kmT[ci][:, :n], t_km[ci])
        proj_fm(C, wt, t_wt, wkv, [(0, 256), (256, 256)], hT, t_h, 256, post_k)
        wv = [C.sb(ph, "mwv", [128, DC, 512], BF16) for _ in range(2)]
        t_wv = toks(2)

        def post_v(g, tt, w, pp, tpp):
            S.add("act", lambda e: e.activation(out=vm[tt][:], in_=pp, func=AF.Copy), reads=[tpp], writes=[t_vm[tt]])
        proj_tm(C, wv, t_wv, wkv, [(512, 512)], hT, t_h, 256, post_v)
    S.barrier()
    return kmT, vm, t_km, t_vm


def mem_attn(C, ph, qm, t_qm, kmT, vm, t_km, t_vm, ntok, ydst, t_y):
    S = C.S
    E = [C.sb(ph, "mE", [128, 512], BF16) for _ in range(2)]
    rr = C.sb(ph, "mrr", [128, 512], F32)
    t_E = toks(2)
    t_rr = Tok()
    cnt = 0
    for hm in range(4):
        for hh in range(ntok // 512):
            cs = slice(hh * 512, (hh + 1) * 512)
            pnum, tnum = C.psum[5], C.ptok[5]
            pden, tden = C.psum[6], C.ptok[6]
            for kt in range(2):
                pb = cnt % 2
                s = cnt % 2
                cnt += 1
                ps, tps = C.psum[pb], C.ptok[pb]
                S.add("pe", lambda e, ps=ps, hm=hm, kt=kt, cs=cs: e.matmul(
                    ps[:], lhsT=kmT[hm][:, kt * 128:(kt + 1) * 128], rhs=qm[hm][:, cs], start=True, stop=True),
                    reads=[t_km[hm], t_qm[hm]], writes=[tps])
                S.add("act", lambda e, ps=ps, s=s: e.activation(out=E[s][:], in_=ps[:], func=AF.Exp), reads=[tps], writes=[t_E[s]])
                S.add("pe", lambda e, s=s, hm=hm, kt=kt, pnum=pnum: e.matmul(
                    pnum[:], lhsT=vm[kt][:, hm * 128:(hm + 1) * 128], rhs=E[s][:], start=(kt == 0), stop=(kt == 1)),
                    reads=[t_vm[kt], t_E[s]], writes=[tnum])
                S.add("pe", lambda e, s=s, kt=kt, pden=pden: e.matmul(
                    pden[:], lhsT=C.ones1, rhs=E[s][:], start=(kt == 0), stop=(kt == 1)),
                    reads=[t_E[s], C.t_c], writes=[tden])
            S.add("dve", lambda e, pden=pden: e.reciprocal(out=rr[:], in_=pden[:]), reads=[tden], writes=[t_rr])
            S.add("dve", lambda e, pnum=pnum, hm=hm, cs=cs: e.tensor_tensor(out=ydst(hm, cs), in0=pnum[:], in1=rr[:], op=ALU.mult),
                  reads=[tnum, t_rr], writes=[t_y(hm, hh)])


def in_proj_A(C, xT, t_x, col0, w_in, kmv, qT_o, kT_o, V_o, ymT_o):
    S = C.S
    kmT, vm, t_km, t_vm = kmv
    nh = NT // 512
    with ExitStack() as ph:
        hT = C.sb(ph, "ihT", [128, DC, NT], BF16)
        t_h = toks(nh)
        g0 = PAR[("mix_g", 0)]
        norm_to_hT(C, ph, xT, t_x, C.par[:, g0:g0 + 16], hT, t_h, NT, 7)
        wt = [C.sb(ph, "iwt", [128, DC, 256], BF16) for _ in range(2)]
        t_wt = toks(2)
        R = NormRes(C, ph)
        stg = [C.sb(ph, "istg", [128, NT], BF16) for _ in range(2)]
        t_stg = toks(2)
        qm = [C.sb(ph, "iqm", [128, NT], BF16) for _ in range(4)]
        t_qm = toks(4)

        def post(ci, hh, n, pp, tpp):
            cs = slice(hh * 512, hh * 512 + n)
            if ci < 24:
                s = ci % 2
                gcol = C.dcol("a_gq2s") if ci < 12 else C.col("a_gk2")
                post_norm(C, R, pp, tpp, n, C.blk64, gcol, stg[s][:, cs], t_stg[s])
                if hh == nh - 1:
                    dst = qT_o[ci, :, col0:col0 + NT] if ci < 12 else kT_o[ci - 12, :, col0:col0 + NT]
                    ot = Tok()
                    C.out_toks.append(ot)
                    S.add("sp", lambda e: e.dma_start(out=dst, in_=stg[s][:]), reads=[t_stg[s]], writes=[ot], dma=True)
            else:
                hm = ci - 24
                post_norm(C, R, pp, tpp, n, C.ones128, C.dcol("mem_gq_s0"), qm[hm][:, cs], t_qm[hm])

        groups = [(c0, 256) for c0 in range(0, 3072, 256)] + [(4608, 256), (4864, 256)]
        proj_fm(C, wt, t_wt, w_in, groups, hT, t_h, NT, post)

        wv = [C.sb(ph, "iwv", [128, DC, 512], BF16) for _ in range(2)]
        t_wv = toks(2)
        vst = [C.sb(ph, "ivst", [128, 512], BF16) for _ in range(2)]
        t_vst = toks(2)
        vc = [0]

        def post_v(g, tt, w, pp, tpp):
            s = vc[0] % 2
            vc[0] += 1
            S.add("act", lambda e: e.activation(out=vst[s][:], in_=pp, func=AF.Copy), reads=[tpp], writes=[t_vst[s]])
            ot = Tok()
            C.out_toks.append(ot)
            r0 = col0 + tt * 128
            S.add("sp", lambda e: e.dma_start(out=V_o[g * 4:(g + 1) * 4, r0:r0 + 128, :].rearrange("h t c -> t h c"),
                                              in_=vst[s][:].rearrange("t (h c) -> t h c", h=4)),
                  reads=[t_vst[s]], writes=[ot], dma=True)
        proj_tm(C, wv, t_wv, w_in, [(3072, 512), (3584, 512), (4096, 512)], hT, t_h, NT, post_v)

        ym = [C.sb(ph, "iym", [128, NT], BF16) for _ in range(4)]
        t_ym = toks(4)
        mem_attn(C, ph, qm, t_qm, kmT, vm, t_km, t_vm, NT, lambda hm, cs: ym[hm][:, cs], lambda hm, hh: t_ym[hm])
        for hm in range(4):
            ot = Tok()
            C.out_toks.append(ot)
            S.add("sp", lambda e, hm=hm: e.dma_start(out=ymT_o[hm, :, col0:col0 + NT], in_=ym[hm][:]),
                  reads=[t_ym[hm]], writes=[ot], dma=True)
    S.barrier()


def own_rows(a, hf):
    return np.ascontiguousarray(a.reshape(16, 2, 128, *a.shape[1:])[:, hf].reshape(NTOK, *a.shape[1:]))


def load_seq(C, eng, dst, src_all, t_dst):
    for r in range(2):
        C.S.add(eng, lambda e, r=r: e.dma_start(
            out=dst.rearrange("p (l r t) -> p l r t", r=2, t=128)[:, :, r, :],
            in_=src_all[r].rearrange("p (l t) -> p l t", t=128)),
            writes=[t_dst], dma=True)


def load_vseq(C, eng, dst, v_of_rank, t_dst):
    for r in range(2):
        C.S.add(eng, lambda e, r=r: e.dma_start(
            out=dst.rearrange("p (l r) w -> p l r w", r=2)[:, :, r, :],
            in_=v_of_rank(r).rearrange("(l p) w -> p l w", p=128)),
            writes=[t_dst], dma=True)


def prep_wt(C, st, wt_dram):
    S = C.S
    C.WTs = C.sb(st, "wts", [128, 3, 12, 128], BF16)
    C.t_wt = Tok()
    with ExitStack() as tmp:
        w32 = C.sb(tmp, "wt32", [128, 3, 12, 128], F32)
        t0 = Tok()
        S.add("sp", lambda e: e.dma_start(out=w32[:], in_=wt_dram), writes=[t0], dma=True)
        for h in range(12):
            S.add("dve", lambda e, h=h: e.tensor_scalar(
                out=C.WTs[:, :, h, :], in0=w32[:, :, h, :], scalar1=C.par[:, PAR["b31"] + h:PAR["b31"] + h + 1],
                scalar2=None, op0=ALU.subtract), reads=[t0, C.t_par], writes=[C.t_wt])
    S.barrier()


def chunk_keys(ci):
    for jk in range(8 * ci + 8):
        yield jk, max(0, (jk) // 2 - 4 * ci)


def window_adds(C, ps, tps, jk, ci, h, istart):
    n = 0
    for w in range(3):
        if (jk + 1 - w) % 2:
            continue
        i = (jk + 1 - w) // 2 - 4 * ci
        if i < istart or i > 3:
            continue
        C.S.add("pe", lambda e, i=i, w=w: e.matmul(ps[:, i * 128:(i + 1) * 128], lhsT=C.ident, rhs=C.WTs[:, w, h, :],
                                                   start=False, stop=False, skip_group_check=True),
                reads=[C.t_wt, C.t_c], writes=[tps])
        n += 1
    return n


def diff_attention(C, qT_s, kT_all, V_all, yT_s):
    S = C.S
    with ExitStack() as ph:
        kh = [C.sb(ph, "kh", [128, T], BF16) for _ in range(2)]
        vh = [C.sb(ph, "vh", [128, 32, 128], BF16) for _ in range(2)]
        qh = [C.sb(ph, "qh", [128, NTOK], BF16) for _ in range(2)]
        t_kh, t_vh, t_qh = toks(2), toks(2), toks(2)
        E = [C.sb(ph, "E", [128, 512], BF16) for _ in range(3)]
        t_E = toks(3)
        o = [C.sb(ph, "o", [128, 512], F32) for _ in range(2)]
        rr = [C.sb(ph, "rr", [128, 512], F32) for _ in range(2)]
        t_o, t_rr = toks(2), toks(2)
        of = C.sb(ph, "of", [128, 512], F32)
        t_of = Tok()
        R = NormRes(C, ph, banks=(7,))
        yst = [C.sb(ph, "yst", [128, 512], BF16) for _ in range(2)]
        t_yst = toks(2)
        cnt = 0
        fin = 0

        def load_head(h):
            s = h % 2
            load_seq(C, "sp", kh[s][:], kT_all[:, h], t_kh[s])
            load_vseq(C, "sp", vh[s][:], lambda r, h=h: V_all[h, r], t_vh[s])
            S.add("sp", lambda e: e.dma_start(out=qh[s][:], in_=qT_s[h]), writes=[t_qh[s]], dma=True)

        load_head(0)
        for h in range(12):
            if h + 1 < 12:
                load_head(h + 1)
            s = h % 2
            for ci in range(4):
                q0 = ci * 512
                keys = list(chunk_keys(ci))
                for jk, istart in keys:
                    c0 = istart * 128
                    n = 512 - c0
                    for m in range(2):
                        pb = cnt % 3
                        cnt += 1
                        ps, tps = C.psum[pb], C.ptok[pb]
                        ms = slice(m * 64, (m + 1) * 64)
                        S.add("pe", lambda e, ps=ps, ms=ms, jk=jk, c0=c0, s=s, q0=q0: e.matmul(
                            ps[:, c0:512], lhsT=kh[s][ms, jk * 128:(jk + 1) * 128], rhs=qh[s][ms, q0 + c0:q0 + 512],
                            start=True, stop=False, skip_group_check=True),
                            reads=[t_kh[s], t_qh[s]], writes=[tps])
                        window_adds(C, ps, tps, jk, ci, h, istart)
                        Es, tE = E[pb], t_E[pb]
                        S.add("act", lambda e, ps=ps, Es=Es, c0=c0, n=n, h=h: e.activation(
                            out=Es[:, :n], in_=ps[:, c0:512], func=AF.Exp,
                            bias=C.par[:, PAR["b31"] + h:PAR["b31"] + h + 1], scale=1.0),
                            reads=[tps, C.t_par], writes=[tE])
                        pnum, tnum = C.psum[3 + m], C.ptok[3 + m]
                        pden, tden = C.psum[5 + m], C.ptok[5 + m]
                        first, last = (jk == 0), (jk == keys[-1][0])
                        S.add("pe", lambda e, pnum=pnum, Es=Es, jk=jk, c0=c0, n=n, first=first, last=last, s=s: e.matmul(
                            pnum[:, c0:512], lhsT=vh[s][:, jk, :], rhs=Es[:, :n], start=first, stop=last, skip_group_check=True),
                            reads=[t_vh[s], tE], writes=[tnum])
                        S.add("pe", lambda e, pden=pden, Es=Es, c0=c0, n=n, first=first, last=last: e.matmul(
                            pden[:, c0:512], lhsT=C.ones1, rhs=Es[:, :n], start=first, stop=last, skip_group_check=True),
                            reads=[tE, C.t_c], writes=[tden])
                for m in range(2):
                    S.add("dve", lambda e, m=m: e.reciprocal(out=rr[m][:], in_=C.psum[5 + m][:]), reads=[C.ptok[5 + m]], writes=[t_rr[m]])
                    S.add("dve", lambda e, m=m: e.tensor_tensor(out=o[m][:], in0=C.psum[3 + m][:], in1=rr[m][:], op=ALU.mult),
                          reads=[C.ptok[3 + m], t_rr[m]], writes=[t_o[m]])
                S.add("dve", lambda e: e.scalar_tensor_tensor(out=of[:], in0=o[1][:], scalar=C.dcol("neglam"), in1=o[0][:],
                                                              op0=ALU.mult, op1=ALU.add),
                      reads=[t_o[0], t_o[1], C.t_par], writes=[t_of])
                ys = fin % 2
                fin += 1
                post_norm(C, R, of[:], t_of, 512, C.ones128, C.dcol("a_gsub_s"), yst[ys][:], t_yst[ys])
                ot = Tok()
                C.scr_toks.setdefault("yT", []).append(ot)
                S.add("sp", lambda e, ys=ys, q0=q0, h=h: e.dma_start(out=yT_s[h, :, q0:q0 + 512], in_=yst[ys][:]),
                      reads=[t_yst[ys]], writes=[ot], dma=True)
    S.barrier()


def out_proj(C, xT, t_x, yT_s, col0, w_out):
    S = C.S
    nh = NT // 512
    with ExitStack() as ph:
        yT = C.sb(ph, "oyT", [128, DC, NT], BF16)
        t_y = toks(nh)
        for c in range(DC):
            S.add("sp", lambda e, c=c: e.dma_start(out=yT[:, c, :], in_=yT_s[c, :, col0:col0 + NT]), writes=t_y, dma=True)
        wt = [C.sb(ph, "owt", [128, DC, 256], BF16) for _ in range(2)]
        t_wt = toks(2)

        def post(ci, hh, n, pp, tpp):
            cs = slice(hh * 512, hh * 512 + n)
            S.add("dve", lambda e: e.tensor_tensor(out=xT[:, ci, cs], in0=pp, in1=xT[:, ci, cs], op=ALU.add),
                  reads=[tpp, t_x[ci][hh]], writes=[t_x[ci][hh]])
        proj_fm(C, wt, t_wt, w_out, [(c0, 256) for c0 in range(0, D, 256)], yT, t_y, NT, post)
    S.barrier()


PAIRS = [[0, 1], [2, 3], [4, 5], [6, 7]]
ALL8 = [list(range(8))]

WEIGHTS = [
    ("ffn1_w_gate", 0, D, FF), ("ffn1_w_up", 0, D, FF), ("ffn1_w_down", 0, FF, D),
    ("mem_w_kv", 0, D, 1024), ("a_w_in", 0, D, 5120), ("w_out", 0, D, D),
    ("ffn2_w_gate", 0, D, FF), ("ffn2_w_up", 0, D, FF), ("ffn2_w_down", 0, FF, D),
    ("ffn1_w_gate", 1, D, FF), ("ffn1_w_up", 1, D, FF), ("ffn1_w_down", 1, FF, D),
    ("mem_w_kv", 1, D, 1024), ("b_w_in", 0, D, 4176), ("w_out", 1, D, D),
    ("ffn2_w_gate", 1, D, FF), ("ffn2_w_up", 1, D, FF), ("ffn2_w_down", 1, FF, D),
]


def wname(k, i):
    return f"{k}_{i}"


def dram_copy(C, dst, src, rows, t_dst, nsplit=4, reads=()):
    step = (rows + nsplit - 1) // nsplit
    for r0 in range(0, rows, step):
        r1 = min(rows, r0 + step)
        C.S.add("sp", lambda e, r0=r0, r1=r1: e.dma_start(out=dst[r0:r1, :], in_=src[r0:r1, :]),
                reads=list(reads), writes=[t_dst], dma=True)


def allgather(C, src, dst, groups, t_src, t_dst):
    C.S.add("pool", lambda e: e.collective_compute("AllGather", ALU.bypass, replica_groups=groups,
                                                   ins=[src.opt()], outs=[dst.opt()]),
            reads=[t_src], writes=[t_dst], dma=True, cc=True)


def gather_weights(C, nlayers):
    W = {}
    chain = []
    for k, i, rows, cols in WEIGHTS:
        if i >= nlayers or (k == "b_w_in" and nlayers < 2):
            continue
        nm = wname(k, i)
        rs = rows // 8
        shard = C.inp(nm, [rs, cols])
        src = C.scratch(nm + "_s", [rs, cols], BF16)
        full = C.nc.dram_tensor(nm + "_f", [rows, cols], BF16, kind="Internal", addr_space="Shared").ap()
        t_s, t_f = Tok(), Tok()
        step = 64
        for r0 in range(0, rs, step):
            r1 = min(rs, r0 + step)
            C.S.add("pool", lambda e, r0=r0, r1=r1, src=src, shard=shard: e.dma_start(out=src[r0:r1, :], in_=shard[r0:r1, :]),
                    writes=[t_s], dma=True)
        C.S.add("pool", lambda e, src=src, full=full: e.collective_compute(
            "AllGather", ALU.bypass, replica_groups=ALL8, ins=[src.opt()], outs=[full.opt()]),
            reads=[t_s] + ([chain[-3]] if len(chain) >= 3 else []), writes=[t_f], dma=True, cc=True)
        chain.append(t_f)
        W[nm] = WT_(full, t_f)
    return W


def build_program(nlayers=2, stop=0):
    nc = bass.Bass("TRN2", target_bir_lowering=False)
    with ExitStack() as st:
        C = Ctx(nc, st)
        S = C.S
        x = C.inp("x", [NTOK, D])
        mem = C.inp("mem", [256, D])
        cst = C.inp("cst", [128, 640])
        par = C.inp("par", [128, NPAR])
        wtab = C.inp("wtab", [128, 3, 12, 128])
        cmt = C.inp("cmt", [128, 3, 128])
        y = C.out("y", [NTOK, D])
        C.consts(cst)
        load_params(C, par)
        prep_wt(C, st, wtab)
        W = gather_weights(C, nlayers)
        xT_s = C.scratch("xT_s", [DC, 128, NTOK], F32)
        qT_s = C.scratch("qT_s", [12, 128, NTOK], BF16)
        yT_s = C.scratch("yT_s", [DC, 128, NTOK], BF16)
        kx0 = C.scratch("kx0", [12 * 128, NTOK], BF16)
        vx0 = C.scratch("vx0", [12 * NTOK, 128], BF16)
        ka0 = C.scratch("ka0", [12 * 2 * 128, NTOK], BF16)
        va0 = C.scratch("va0", [12 * 2 * NTOK, 128], BF16)
        kx1 = C.scratch("kx1", [5 * 128, NTOK], BF16)
        vx1 = C.scratch("vx1", [4 * NTOK, 128], BF16)
        ka1 = C.scratch("ka1", [5 * 2 * 128, NTOK], BF16)
        va1 = C.scratch("va1", [4 * 2 * NTOK, 128], BF16)

        def exchange(kx, ka, nk, vx, va, nv):
            prev = Tok()
            for h in range(nk):
                t1 = Tok()
                allgather(C, kx[h * 128:(h + 1) * 128, :], ka[h * 256:(h + 1) * 256, :], PAIRS, prev, t1)
                prev = t1
            for h in range(nv):
                t1 = Tok()
                allgather(C, vx[h * NTOK:(h + 1) * NTOK, :], va[h * 2 * NTOK:(h + 1) * 2 * NTOK, :], PAIRS, prev, t1)
                prev = t1
            S.barrier()
        iqT_s = C.scratch("iqT_s", [8, 128, NTOK], BF16)
        iw_s = C.scratch("iw_s", [NTOK, 16], F32)
        gcl = lambda n, l: C.par[:, PAR[(n, l)]:PAR[(n, l)] + 16]
        with ExitStack() as lay:
            kmv = mem_kv(C, lay, mem, W["mem_w_kv_0"], 0)
            xT = C.sb(lay, "xT", [128, DC, NT], F32)
            for tt in range(NTOK // NT):
                t_x = [toks(NT // 512) for _ in range(DC)]
                col0 = tt * NT
                load_x_tokmajor(C, x[col0:col0 + NT, :], xT, t_x, NT)
                ffn(C, xT, t_x, gcl("ffn1_g", 0), W["ffn1_w_gate_0"], W["ffn1_w_up_0"], W["ffn1_w_down_0"], NT)
                in_proj_A(C, xT, t_x, col0, W["a_w_in_0"], kmv, qT_s,
                          kx0.rearrange("(h p) t -> h p t", p=128), vx0.rearrange("(h t) c -> h t c", h=12), yT_s[12:16])
                store_xT(C, xT_s, xT, t_x, NT, col0)
                S.barrier()
        for layer in range(nlayers):
            t_ka, t_va, t0 = Tok(), Tok(), Tok()
            if stop == 1:
                break
            if layer == 0:
                exchange(kx0, ka0, 12, vx0, va0, 12)
                if stop == 2:
                    break
                diff_attention(C, qT_s, ka0.rearrange("(h r p) t -> r h p t", r=2, p=128),
                               va0.rearrange("(h r t) c -> h r t c", h=12, r=2), yT_s)
            else:
                exchange(kx1, ka1, 5, vx1, va1, 4)
                dsa_attention(C, qT_s, iqT_s, iw_s, ka1.rearrange("(h r p) t -> r h p t", r=2, p=128),
                              va1.rearrange("(h r t) c -> h r t c", h=4, r=2), cmt, yT_s)
            if stop == 3:
                break
            with ExitStack() as lay:
                last = layer == nlayers - 1
                if not last:
                    kmv = mem_kv(C, lay, mem, W[f"mem_w_kv_{layer + 1}"], layer + 1)
                xT = C.sb(lay, "xT", [128, DC, NT], F32)
                for tt in range(NTOK // NT):
                    t_x = [toks(NT // 512) for _ in range(DC)]
                    col0 = tt * NT
                    load_xT(C, xT_s, xT, t_x, NT, col0)
                    out_proj(C, xT, t_x, yT_s, col0, W[f"w_out_{layer}"])
                    ffn(C, xT, t_x, gcl("ffn2_g", layer), W[f"ffn2_w_gate_{layer}"], W[f"ffn2_w_up_{layer}"],
                        W[f"ffn2_w_down_{layer}"], NT)
                    if last:
                        store_x_tokmajor(C, xT, t_x, y[col0:col0 + NT, :], NT)
                    else:
                        ffn(C, xT, t_x, gcl("ffn1_g", layer + 1), W[f"ffn1_w_gate_{layer + 1}"], W[f"ffn1_w_up_{layer + 1}"],
                            W[f"ffn1_w_down_{layer + 1}"], NT)
                        in_proj_B(C, xT, t_x, col0, W["b_w_in_0"], kmv, qT_s,
                                  kx1.rearrange("(h p) t -> h p t", p=128), vx1.rearrange("(h t) c -> h t c", h=4), iqT_s, iw_s, yT_s[12:16])
                        store_xT(C, xT_s, xT, t_x, NT, col0)
                    S.barrier()
        print("program", finish(C))
    return nc


def bucket_np(n):
    n = np.maximum(n, 0)
    nf = np.maximum(n, 1).astype(np.float32)
    large = 16 + (np.log(nf / 16) / math.log(128 / 16) * 16).astype(np.int32)
    large = np.minimum(large, 31)
    return np.where(n < 16, n, large)


def window_tables(rel_bias, hf):
    s = np.arange(128)[:, None]
    t = np.arange(128)[None, :]
    tabs = {}
    d = t - s
    tabs["diag"] = np.where((d >= 0)[None], rel_bias[bucket_np(d)].transpose(2, 0, 1), NEG)
    tabs["off"] = rel_bias[bucket_np(128 + d)].transpose(2, 0, 1)
    tabs["far"] = np.broadcast_to(rel_bias[31][:, None, None], (12, 128, 128))
    tabs["masked"] = np.full((12, 128, 128), NEG, np.float32)
    order = ("off", "diag", "masked") if hf == 0 else ("far", "off", "diag")
    out = np.stack([tabs[o] for o in order], 0)
    return np.ascontiguousarray(out.transpose(2, 0, 1, 3)).astype(np.float32)


_NLAYERS = 2


def kernel(**inp):
    inp = {k: np.asarray(v) for k, v in inp.items()}
    nc = build_program(_NLAYERS)
    cst, par = const_array(), pack_params(inp)
    maps = []
    for c in range(8):
        b, hf = c // 2, c % 2
        m = {"x": own_rows(inp["x"][b], hf), "mem": np.ascontiguousarray(inp["mem"][b]), "cst": cst, "par": par,
             "wtab": window_tables(inp["rel_bias"], hf), "cmt": causal_tables(hf)}
        for k, i, rows, cols in WEIGHTS:
            if i >= _NLAYERS or (k == "b_w_in" and _NLAYERS < 2):
                continue
            r = rows // 8
            m[wname(k, i)] = np.ascontiguousarray(inp[k][i][c * r:(c + 1) * r])
        maps.append(m)
    res = run_bass_kernel_spmd(nc, maps, core_ids=list(range(8))).results
    out = np.zeros((4, T, D), np.float32)
    for c in range(8):
        b, hf = c // 2, c % 2
        out[b].reshape(16, 2, 128, D)[:, hf] = res[c]["y"].reshape(16, 128, D)
    return out


def in_proj_B(C, xT, t_x, col0, w_in, kmv, qT_s, kx1, vx1, iqT_s, iw_s, ymT_o):
    S = C.S
    kmT, vm, t_km, t_vm = kmv
    nh = NT // 512
    with ExitStack() as ph:
        hT = C.sb(ph, "ihT", [128, DC, NT], BF16)
        t_h = toks(nh)
        g0 = PAR[("mix_g", 1)]
        norm_to_hT(C, ph, xT, t_x, C.par[:, g0:g0 + 16], hT, t_h, NT, 7)
        wt = [C.sb(ph, "iwt", [128, DC, 256], BF16) for _ in range(2)]
        t_wt = toks(2)
        R = NormRes(C, ph)
        stg = [C.sb(ph, "istg", [128, NT], BF16) for _ in range(2)]
        t_stg = toks(2)
        qm = [C.sb(ph, "iqm", [128, NT], BF16) for _ in range(4)]
        t_qm = toks(4)

        def flush(s, dst):
            ot = Tok()
            C.out_toks.append(ot)
            S.add("sp", lambda e: e.dma_start(out=dst, in_=stg[s][:]), reads=[t_stg[s]], writes=[ot], dma=True)

        def post(ci, hh, n, pp, tpp):
            cs = slice(hh * 512, hh * 512 + n)
            s = ci % 2
            if ci < 16:
                gcol = C.dcol("b_gq_s") if ci < 12 else C.col("b_gk")
                post_norm(C, R, pp, tpp, n, C.ones128, gcol, stg[s][:, cs], t_stg[s])
                if hh == nh - 1:
                    flush(s, qT_s[ci, :, col0:col0 + NT] if ci < 12 else kx1[ci - 12, :, col0:col0 + NT])
            elif ci < 24:
                S.add("act", lambda e: e.activation(out=stg[s][:, cs], in_=pp, func=AF.Copy), reads=[tpp], writes=[t_stg[s]])
                if hh == nh - 1:
                    flush(s, iqT_s[ci - 16, :, col0:col0 + NT])
            else:
                hm = ci - 24
                post_norm(C, R, pp, tpp, n, C.ones128, C.dcol("mem_gq_s1"), qm[hm][:, cs], t_qm[hm])

        groups = ([(c0, 256) for c0 in range(0, 2048, 256)] + [(c0, 256) for c0 in range(2560, 3584, 256)]
                  + [(3664, 256), (3920, 256)])
        proj_fm(C, wt, t_wt, w_in, groups, hT, t_h, NT, post)

        for half in range(2):
            load_w(C, "pool", wt[0][:, :, half * 64:(half + 1) * 64], w_in, 3584, 64, t_wt[0])
        for hh in range(nh):
            cs = slice(hh * 512, (hh + 1) * 512)
            pp, tpp = C.psum[hh % 2], C.ptok[hh % 2]
            for k in range(DC):
                S.add("pe", lambda e, k=k, pp=pp, cs=cs: e.matmul(pp[:], lhsT=wt[0][:, k, 0:128], rhs=hT[:, k, cs],
                                                                  start=(k == 0), stop=(k == DC - 1)),
                      reads=[t_wt[0], t_h[hh]], writes=[tpp])
            S.add("act", lambda e, pp=pp, cs=cs: e.activation(out=stg[0][:, cs], in_=pp[:], func=AF.Copy), reads=[tpp], writes=[t_stg[0]])
        flush(0, kx1[4, :, col0:col0 + NT])

        wv = [C.sb(ph, "iwv", [128, DC, 512], BF16) for _ in range(2)]
        t_wv = toks(2)
        vst = [C.sb(ph, "ivst", [128, 512], BF16) for _ in range(2)]
        t_vst = toks(2)
        iwst = [C.sb(ph, "iwst", [128, 16], F32) for _ in range(2)]
        t_iwst = toks(2)
        vc = [0]

        def post_v(g, tt, w, pp, tpp):
            s = vc[0] % 2
            vc[0] += 1
            S.add("act", lambda e: e.activation(out=vst[s][:], in_=pp, func=AF.Copy), reads=[tpp], writes=[t_vst[s]])
            ot = Tok()
            C.out_toks.append(ot)
            r0 = col0 + tt * 128
            S.add("sp", lambda e: e.dma_start(out=vx1[:, r0:r0 + 128, :].rearrange("h t c -> t h c"),
                                              in_=vst[s][:].rearrange("t (h c) -> t h c", h=4)),
                  reads=[t_vst[s]], writes=[ot], dma=True)
        proj_tm(C, wv, t_wv, w_in, [(2048, 512)], hT, t_h, NT, post_v)

        def post_iw(g, tt, w, pp, tpp):
            s = vc[0] % 2
            vc[0] += 1
            S.add("dve", lambda e: e.tensor_scalar(out=iwst[s][:], in0=pp, scalar1=0.25 * 0.125, scalar2=None, op0=ALU.mult),
                  reads=[tpp], writes=[t_iwst[s]])
            ot = Tok()
            C.out_toks.append(ot)
            r0 = col0 + tt * 128
            S.add("sp", lambda e: e.dma_start(out=iw_s[r0:r0 + 128, :], in_=iwst[s][:]), reads=[t_iwst[s]], writes=[ot], dma=True)
        proj_tm(C, wv, t_wv, w_in, [(3648, 16)], hT, t_h, NT, post_iw)

        ym = [C.sb(ph, "iym", [128, NT], BF16) for _ in range(4)]
        t_ym = toks(4)
        mem_attn(C, ph, qm, t_qm, kmT, vm, t_km, t_vm, NT, lambda hm, cs: ym[hm][:, cs], lambda hm, hh: t_ym[hm])
        for hm in range(4):
            ot = Tok()
            C.out_toks.append(ot)
            S.add("sp", lambda e, hm=hm: e.dma_start(out=ymT_o[hm, :, col0:col0 + NT], in_=ym[hm][:]),
                  reads=[t_ym[hm]], writes=[ot], dma=True)
    S.barrier()


NEGBIG = -1.0e30


def dsa_attention(C, qT_s, iqT_s, iw_s, ka1, va1, cm_dram, yT_s):
    S = C.S
    with ExitStack() as ph:
        kseq = [C.sb(ph, "kseq", [128, T], BF16) for _ in range(4)]
        t_kseq = toks(4)
        for g in range(4):
            load_seq(C, "sp", kseq[g][:], ka1[:, g], t_kseq[g])
        ikseq = C.sb(ph, "ikseq", [128, T], BF16)
        t_ik = Tok()
        load_seq(C, "sp", ikseq[:], ka1[:, 4], t_ik)
        vseq = C.sb(ph, "vseq", [128, 32, 512], BF16)
        t_vs = Tok()
        for g in range(4):
            load_vseq(C, "sp", vseq[:, :, g * 128:(g + 1) * 128], lambda r, g=g: va1[g, r], t_vs)
        cm = C.sb(ph, "cm", [128, 3, 128], F32)
        t_cm = Tok()
        S.add("sp", lambda e: e.dma_start(out=cm[:], in_=cm_dram), writes=[t_cm], dma=True)
        iwt = C.sb(ph, "iwt", [128, NLT, 16], F32)
        t_iw = Tok()
        S.add("sp", lambda e: e.dma_start(out=iwt[:], in_=iw_s.rearrange("(l p) j -> p l j", p=128)), writes=[t_iw], dma=True)
        acc = C.sb(ph, "acc", [128, T], F32)
        work = C.sb(ph, "work", [128, T], F32)
        m8 = C.sb(ph, "m8", [128, 8], F32)
        thr = C.sb(ph, "thr", [128, 1], F32)
        t_acc, t_work, t_m8, t_thr = Tok(), Tok(), Tok(), Tok()
        mb = [C.sb(ph, "mb", [128, T], BF16) for _ in range(4)]
        t_mb = toks(4)
        rl = [C.sb(ph, "rl", [128, 512], F32) for _ in range(2)]
        t_rl = toks(2)
        iqh = [C.sb(ph, "iqh", [128, 8, 128], BF16) for _ in range(2)]
        t_iqh = toks(2)
        qh = [C.sb(ph, "dqh", [128, 512], BF16) for _ in range(2)]
        t_qh = toks(2)
        E = [C.sb(ph, "dE", [128, 512], BF16) for _ in range(3)]
        t_E = toks(3)
        rr = C.sb(ph, "drr", [128, 512], F32)
        t_rr = Tok()
        yst = [C.sb(ph, "dyst", [128, 512], BF16) for _ in range(2)]
        t_yst = toks(2)
        ca = cb = cq = cy = 0
        for ci in range(4):
            for i in range(4):
                lq = 4 * ci + i
                L = (2 * lq + 2) * 128
                si = lq % 2
                S.add("sp", lambda e, si=si, lq=lq: e.dma_start(
                    out=iqh[si][:], in_=iqT_s[:, :, lq * 128:(lq + 1) * 128].rearrange("c p t -> p c t")),
                    writes=[t_iqh[si]], dma=True)
                for kb in range((L + 511) // 512):
                    nk = min(512, L - kb * 512)
                    ks = slice(kb * 512, kb * 512 + nk)
                    for j in range(16):
                        c, half = j // 2, j % 2
                        hs = slice(half * 64, half * 64 + 64)
                        pb = ca % 2
                        ca += 1
                        ps, tps = C.psum[pb], C.ptok[pb]
                        S.add("pe", lambda e, ps=ps, hs=hs, c=c, si=si, ks=ks, nk=nk: e.matmul(
                            ps[:, :nk], lhsT=iqh[si][hs, c, :], rhs=ikseq[hs, ks], start=True, stop=True),
                            reads=[t_iqh[si], t_ik], writes=[tps])
                        S.add("act", lambda e, ps=ps, pb=pb, nk=nk: e.activation(out=rl[pb][:, :nk], in_=ps[:, :nk], func=AF.Relu),
                              reads=[tps], writes=[t_rl[pb]])
                        wcol = iwt[:, lq, j:j + 1]
                        if j == 0:
                            S.add("dve", lambda e, pb=pb, ks=ks, nk=nk, wcol=wcol: e.tensor_scalar(
                                out=acc[:, ks], in0=rl[pb][:, :nk], scalar1=wcol, scalar2=None, op0=ALU.mult),
                                reads=[t_rl[pb], t_iw], writes=[t_acc])
                        else:
                            S.add("dve", lambda e, pb=pb, ks=ks, nk=nk, wcol=wcol: e.scalar_tensor_tensor(
                                out=acc[:, ks], in0=rl[pb][:, :nk], scalar=wcol, in1=acc[:, ks], op0=ALU.mult, op1=ALU.add),
                                reads=[t_rl[pb], t_iw, t_acc], writes=[t_acc])
                for w in range(3):
                    jk = 2 * lq - 1 + w
                    if jk < 0:
                        continue
                    S.add("dve", lambda e, w=w, jk=jk: e.tensor_tensor(
                        out=acc[:, jk * 128:(jk + 1) * 128], in0=acc[:, jk * 128:(jk + 1) * 128], in1=cm[:, w, :], op=ALU.add),
                        reads=[t_acc, t_cm], writes=[t_acc])
                S.add("act", lambda e, L=L: e.activation(out=work[:, :L], in_=acc[:, :L], func=AF.Copy), reads=[t_acc], writes=[t_work])
                for r in range(32):
                    S.add("dve", lambda e, L=L: e.max(out=m8[:], in_=work[:, :L]), reads=[t_work], writes=[t_m8])
                    if r < 31:
                        S.add("dve", lambda e, L=L: e.match_replace(out=work[:, :L], in_to_replace=m8[:], in_values=work[:, :L],
                                                                    imm_value=NEGBIG), reads=[t_m8, t_work], writes=[t_work])
                S.add("dve", lambda e: e.tensor_scalar(out=thr[:], in0=m8[:, 7:8], scalar1=-1.0e29, scalar2=None, op0=ALU.max),
                      reads=[t_m8], writes=[t_thr])
                S.add("dve", lambda e, L=L: e.tensor_scalar(out=work[:, :L], in0=acc[:, :L], scalar1=thr[:, 0:1], scalar2=None, op0=ALU.is_ge),
                      reads=[t_acc, t_thr], writes=[t_work])
                S.add("pool", lambda e, i=i, L=L: e.tensor_scalar(out=mb[i][:, :L], in0=work[:, :L], scalar1=-1.0, scalar2=-NEG,
                                                                 op0=ALU.add, op1=ALU.mult),
                      reads=[t_work], writes=[t_mb[i]])
            q0 = ci * 512
            keys = list(chunk_keys(ci))
            for h in range(12):
                g = h // 3
                sq_ = cq % 2
                cq += 1
                S.add("sp", lambda e, sq_=sq_, h=h, q0=q0: e.dma_start(out=qh[sq_][:], in_=qT_s[h, :, q0:q0 + 512]), writes=[t_qh[sq_]], dma=True)
                pnum, tnum = C.psum[5], C.ptok[5]
                pden, tden = C.psum[6], C.ptok[6]
                for jk, istart in keys:
                    c0 = istart * 128
                    n = 512 - c0
                    pb = 2 + cb % 3
                    es = cb % 3
                    cb += 1
                    ps, tps = C.psum[pb], C.ptok[pb]
                    S.add("pe", lambda e, ps=ps, g=g, jk=jk, c0=c0, sq_=sq_: e.matmul(
                        ps[:, c0:512], lhsT=kseq[g][:, jk * 128:(jk + 1) * 128], rhs=qh[sq_][:, c0:512],
                        start=True, stop=False, skip_group_check=True),
                        reads=[t_kseq[g], t_qh[sq_]], writes=[tps])
                    for i in range(istart, 4):
                        S.add("pe", lambda e, ps=ps, i=i, jk=jk: e.matmul(
                            ps[:, i * 128:(i + 1) * 128], lhsT=mb[i][:, jk * 128:(jk + 1) * 128], rhs=C.ident,
                            start=False, stop=False, skip_group_check=True),
                            reads=[t_mb[i], C.t_c], writes=[tps])
                    window_adds(C, ps, tps, jk, ci, h, istart)
                    Es, tE = E[es], t_E[es]
                    S.add("act", lambda e, ps=ps, Es=Es, c0=c0, n=n, h=h: e.activation(
                        out=Es[:, :n], in_=ps[:, c0:512], func=AF.Exp,
                        bias=C.par[:, PAR["b31"] + h:PAR["b31"] + h + 1], scale=1.0),
                        reads=[tps, C.t_par], writes=[tE])
                    first, last = (jk == 0), (jk == keys[-1][0])
                    S.add("pe", lambda e, Es=Es, jk=jk, g=g, c0=c0, n=n, first=first, last=last: e.matmul(
                        pnum[:, c0:512], lhsT=vseq[:, jk, g * 128:(g + 1) * 128], rhs=Es[:, :n], start=first, stop=last,
                        skip_group_check=True), reads=[t_vs, tE], writes=[tnum])
                    S.add("pe", lambda e, Es=Es, c0=c0, n=n, first=first, last=last: e.matmul(
                        pden[:, c0:512], lhsT=C.ones1, rhs=Es[:, :n], start=first, stop=last, skip_group_check=True),
                        reads=[tE, C.t_c], writes=[tden])
                S.add("dve", lambda e: e.reciprocal(out=rr[:], in_=pden[:]), reads=[tden], writes=[t_rr])
                ys = cy % 2
                cy += 1
                S.add("dve", lambda e, ys=ys: e.tensor_tensor(out=yst[ys][:], in0=pnum[:], in1=rr[:], op=ALU.mult),
                      reads=[tnum, t_rr], writes=[t_yst[ys]])
                ot = Tok()
                C.out_toks.append(ot)
                S.add("sp", lambda e, ys=ys, h=h, q0=q0: e.dma_start(out=yT_s[h, :, q0:q0 + 512], in_=yst[ys][:]),
                      reads=[t_yst[ys]], writes=[ot], dma=True)
    S.barrier()


def causal_tables(hf):
    t = np.arange(128)[:, None]
    s = np.arange(128)[None, :]
    diag = np.where(s <= t, 0.0, NEGBIG).astype(np.float32)
    zero = np.zeros((128, 128), np.float32)
    allm = np.full((128, 128), NEGBIG, np.float32)
    order = (zero, diag, allm) if hf == 0 else (zero, zero, diag)
    return np.ascontiguousarray(np.stack(order, 1))
```
